# Optimizing a Trainium2 kernel written in Bass

```python
import math
import jax, jax.numpy as jnp
from jax import lax
import numpy as np

D_MODEL = 1024
BATCH = 2
SEQ = 8192
DEPTH = 2

N_MIXERS = 2
N_LRU_LAYERS = (DEPTH + 1) // 2
N_FOX_LAYERS = DEPTH // 2
EPS = 1e-6
LRU_WIDTH = 1536
LRU_BLOCKS = 12
LRU_BLOCK_W = LRU_WIDTH // LRU_BLOCKS
CONV_WIDTH = 4
LRU_C = 8.0
FOX_HEADS = 16
FOX_HEAD_DIM = 64
FOX_WIDTH = FOX_HEADS * FOX_HEAD_DIM
Q_BLOCK = 128
NEG_INF = -1e30

kernel_name = "hybrid_rglru_fox_interleaved"


def rms_norm(x, g):
    xf = x.astype(jnp.float32)
    y = xf * lax.rsqrt(jnp.mean(xf * xf, axis=-1, keepdims=True) + EPS)
    return (y * g.astype(jnp.float32)).astype(x.dtype)


def causal_depthwise_conv(x, w, b):
    c = x.shape[-1]
    y = lax.conv_general_dilated(
        x, w[:, None, :].astype(x.dtype), window_strides=(1,),
        padding=[(CONV_WIDTH - 1, 0)],
        dimension_numbers=("NWC", "WIO", "NWC"),
        feature_group_count=c)
    return y + b.astype(x.dtype)


def block_diag_linear(x, w, b):
    bsz, s, _ = x.shape
    xb = x.reshape(bsz, s, LRU_BLOCKS, LRU_BLOCK_W)
    y = jnp.einsum("bsnc,ncd->bsnd", xb, w.astype(x.dtype))
    return y.reshape(bsz, s, LRU_WIDTH) + b.astype(x.dtype)


def lru_mixer(h, w_in, conv_w, conv_b, wa, ba, wx, bx, a_param, w_out):
    u = h @ w_in.astype(h.dtype)
    xb, gate = u[..., :LRU_WIDTH], u[..., LRU_WIDTH:]
    xc = causal_depthwise_conv(xb, conv_w, conv_b)
    r = jax.nn.sigmoid(block_diag_linear(xc, wa, ba).astype(jnp.float32))
    i = jax.nn.sigmoid(block_diag_linear(xc, wx, bx).astype(jnp.float32))
    log_a = -LRU_C * r * jax.nn.softplus(-a_param.astype(jnp.float32))
    a = jnp.exp(log_a)
    mult = jnp.sqrt(-jnp.expm1(2.0 * log_a))
    bterm = mult * (i * xc.astype(jnp.float32))

    def combine(lhs, rhs):
        a1, b1 = lhs
        a2, b2 = rhs
        return a1 * a2, a2 * b1 + b2

    _, hs = lax.associative_scan(combine, (a, bterm), axis=1)
    y = hs.astype(h.dtype) * jax.nn.silu(gate)
    return y @ w_out.astype(h.dtype)


def fox_mixer(h, w_in, b_f, w_out):
    bsz, s, _ = h.shape
    u = h @ w_in.astype(h.dtype)
    q = u[..., 0 * FOX_WIDTH:1 * FOX_WIDTH].reshape(bsz, s, FOX_HEADS, FOX_HEAD_DIM)
    k = u[..., 1 * FOX_WIDTH:2 * FOX_WIDTH].reshape(bsz, s, FOX_HEADS, FOX_HEAD_DIM)
    v = u[..., 2 * FOX_WIDTH:3 * FOX_WIDTH].reshape(bsz, s, FOX_HEADS, FOX_HEAD_DIM)
    gate = u[..., 3 * FOX_WIDTH:4 * FOX_WIDTH]
    f_logit = u[..., 4 * FOX_WIDTH:].astype(jnp.float32) + b_f.astype(jnp.float32)
    cum = jnp.cumsum(jax.nn.log_sigmoid(f_logit), axis=1)
    ck = jnp.transpose(cum, (0, 2, 1))
    scale = 1.0 / math.sqrt(FOX_HEAD_DIM)
    n_blocks = s // Q_BLOCK
    qb = jnp.transpose(q.reshape(bsz, n_blocks, Q_BLOCK, FOX_HEADS, FOX_HEAD_DIM), (1, 0, 2, 3, 4))
    cqb = jnp.transpose(cum.reshape(bsz, n_blocks, Q_BLOCK, FOX_HEADS), (1, 0, 3, 2))
    starts = jnp.arange(n_blocks, dtype=jnp.int32) * Q_BLOCK
    kpos = jnp.arange(s, dtype=jnp.int32)
    kf = k.astype(jnp.float32)
    vf = v.astype(jnp.float32)

    def one_block(args):
        q_blk, cq_blk, start = args
        qpos = start + jnp.arange(Q_BLOCK, dtype=jnp.int32)
        logits = jnp.einsum("bqhd,bkhd->bhqk", q_blk.astype(jnp.float32), kf) * scale
        logits = logits + (cq_blk[..., :, None] - ck[:, :, None, :])
        mask = kpos[None, :] <= qpos[:, None]
        logits = jnp.where(mask[None, None], logits, NEG_INF)
        p = jax.nn.softmax(logits, axis=-1)
        return jnp.einsum("bhqk,bkhd->bqhd", p, vf)

    o = lax.map(one_block, (qb, cqb, starts))
    o = jnp.transpose(o, (1, 0, 2, 3, 4)).reshape(bsz, s, FOX_WIDTH).astype(h.dtype)
    y = o * jax.nn.silu(gate)
    return y @ w_out.astype(h.dtype)


def setup_inputs(seed: int = 0) -> dict:
    key = jax.random.key(seed)
    ks = jax.random.split(key, 16)
    f32 = jnp.float32
    nl, nf = N_LRU_LAYERS, N_FOX_LAYERS
    x = jax.random.normal(ks[0], (BATCH, SEQ, D_MODEL), f32)
    norm_g = 1.0 + 0.05 * jax.random.normal(ks[1], (DEPTH, D_MODEL), f32)
    final_g = 1.0 + 0.05 * jax.random.normal(ks[2], (D_MODEL,), f32)
    lru_w_in = jax.random.normal(ks[3], (nl, D_MODEL, 2 * LRU_WIDTH), f32) * D_MODEL ** -0.5
    lru_conv_w = jax.random.normal(ks[4], (nl, CONV_WIDTH, LRU_WIDTH), f32) * CONV_WIDTH ** -0.5
    lru_conv_b = 0.02 * jax.random.normal(ks[5], (nl, LRU_WIDTH), f32)
    lru_wa = jax.random.normal(ks[6], (nl, LRU_BLOCKS, LRU_BLOCK_W, LRU_BLOCK_W), f32) * LRU_BLOCK_W ** -0.5
    lru_ba = 0.02 * jax.random.normal(ks[7], (nl, LRU_WIDTH), f32)
    lru_wx = jax.random.normal(ks[8], (nl, LRU_BLOCKS, LRU_BLOCK_W, LRU_BLOCK_W), f32) * LRU_BLOCK_W ** -0.5
    lru_bx = 0.02 * jax.random.normal(ks[9], (nl, LRU_WIDTH), f32)
    a_c = jax.random.uniform(ks[10], (nl, LRU_WIDTH), f32, minval=0.9, maxval=0.999)
    a0 = a_c ** (1.0 / LRU_C)
    lru_a_param = jnp.log(a0) - jnp.log1p(-a0)
    lru_w_out = jax.random.normal(ks[11], (nl, LRU_WIDTH, D_MODEL), f32) * LRU_WIDTH ** -0.5
    fox_w_in = jax.random.normal(ks[12], (nf, D_MODEL, 4 * FOX_WIDTH + FOX_HEADS), f32) * D_MODEL ** -0.5
    fox_b_f = 3.0 + 0.5 * jax.random.normal(ks[13], (nf, FOX_HEADS), f32)
    fox_w_out = jax.random.normal(ks[14], (nf, FOX_WIDTH, D_MODEL), f32) * FOX_WIDTH ** -0.5
    return {"x": x, "norm_g": norm_g, "final_g": final_g,
            "lru_w_in": lru_w_in, "lru_conv_w": lru_conv_w, "lru_conv_b": lru_conv_b,
            "lru_wa": lru_wa, "lru_ba": lru_ba, "lru_wx": lru_wx, "lru_bx": lru_bx,
            "lru_a_param": lru_a_param, "lru_w_out": lru_w_out,
            "fox_w_in": fox_w_in, "fox_b_f": fox_b_f, "fox_w_out": fox_w_out}


def reference(x, norm_g, final_g, lru_w_in, lru_conv_w, lru_conv_b, lru_wa, lru_ba,
              lru_wx, lru_bx, lru_a_param, lru_w_out, fox_w_in, fox_b_f, fox_w_out):
    for i in range(DEPTH):
        h = rms_norm(x, norm_g[i])
        j = i // N_MIXERS
        if i % N_MIXERS == 0:
            x = x + lru_mixer(h, lru_w_in[j], lru_conv_w[j], lru_conv_b[j], lru_wa[j], lru_ba[j],
                              lru_wx[j], lru_bx[j], lru_a_param[j], lru_w_out[j])
        else:
            x = x + fox_mixer(h, fox_w_in[j], fox_b_f[j], fox_w_out[j])
    return rms_norm(x, final_g)
```

```python
import contextlib
import numpy as np
import concourse.bass as bass
import concourse.mybir as mybir
from concourse.bass_utils import run_bass_kernel_spmd

F32 = mybir.dt.float32
BF16 = mybir.dt.bfloat16
AF = mybir.ActivationFunctionType
ALU = mybir.AluOpType

D = 1024
S = 8192
W = 1536
EPS = 1e-6
NEG = -30000.0


class Buf:
    __slots__ = ("name", "last_w", "readers", "dma_readers", "sem", "cnt")

    def __init__(self, name):
        self.name = name
        self.last_w = None
        self.readers = {}
        self.dma_readers = []
        self.sem = None
        self.cnt = 0


class Op:
    __slots__ = ("eng", "fn", "deps", "sem", "val", "is_dma", "needs_inc")

    def __init__(self, eng, fn, is_dma=False):
        self.eng = eng
        self.fn = fn
        self.deps = []
        self.sem = None
        self.val = 0
        self.is_dma = is_dma
        self.needs_inc = False


ENGS = ("pe", "act", "dve", "pool", "sp")
SEM_ROT = 30000


class Sched:
    def __init__(self, nc, stack):
        self.nc = nc
        self.stack = stack
        self.ops = []
        self.eng_sem = {}
        self.eng_cnt = {e: 0 for e in ENGS}
        self.sem_final = {}
        self.waited = {e: {} for e in ENGS}
        self.nsem = 0
        for e in ENGS:
            self._new_eng_sem(e)

    def _alloc_sem(self, name):
        self.nsem += 1
        return self.stack.enter_context(self.nc.semaphore(f"{name}_{self.nsem}"))

    def _new_eng_sem(self, e):
        self.eng_sem[e] = self._alloc_sem("e" + e)
        self.eng_cnt[e] = 0

    def _track(self, op, reads, writes):
        deps = []
        seen = set()

        def add(d):
            if d is not None and id(d) not in seen and d is not op:
                seen.add(id(d))
                deps.append(d)

        for b in list(reads) + list(writes):
            add(b.last_w)
        for b in writes:
            for r in b.readers.values():
                add(r)
            for r in b.dma_readers:
                add(r)
        wset = set(id(b) for b in writes)
        for b in writes:
            b.last_w = op
            b.readers = {}
            b.dma_readers = []
        for b in reads:
            if id(b) in wset:
                continue
            if op.is_dma:
                b.dma_readers.append(op)
            else:
                b.readers[op.eng] = op
        for d in deps:
            if d.is_dma:
                continue
            if d.eng == op.eng and d.eng == "pe" and not op.is_dma:
                continue
            d.needs_inc = True
        op.deps = deps

    def op(self, eng, fn, reads=(), writes=()):
        o = Op(eng, fn)
        self._track(o, reads, writes)
        self.ops.append(o)
        return o

    def dma(self, fn, reads=(), writes=(), sem_buf=None, queue="sp"):
        o = Op(queue, fn, is_dma=True)
        self._track(o, reads, writes)
        if sem_buf.sem is None:
            sem_buf.sem = self._alloc_sem("d")
            sem_buf.cnt = 0
        sem_buf.cnt += 1
        o.sem = sem_buf.sem
        o.val = 16 * sem_buf.cnt
        self.sem_final[id(o.sem)] = (o.sem, o.val)
        self.ops.append(o)
        return o

    def flush(self):
        nc = self.nc
        ops = self.ops
        self.ops = []
        for e in ENGS:
            for o in reversed(ops):
                if o.eng == e and not o.is_dma:
                    o.needs_inc = True
                    break
        for o in ops:
            if o.is_dma:
                continue
            if o.needs_inc:
                if self.eng_cnt[o.eng] >= SEM_ROT:
                    self._new_eng_sem(o.eng)
                self.eng_cnt[o.eng] += 1
                o.sem = self.eng_sem[o.eng]
                o.val = self.eng_cnt[o.eng]
                self.sem_final[id(o.sem)] = (o.sem, o.val)
        finals = list(self.sem_final.values())
        per_eng = {e: [o for o in ops if o.eng == e] for e in ENGS}

        def emit(e, eng):
            waited = self.waited[e]
            for o in per_eng[e]:
                for d in o.deps:
                    if d.sem is None:
                        continue
                    if d.eng == e and e == "pe" and not d.is_dma and not o.is_dma:
                        continue
                    k = id(d.sem)
                    if waited.get(k, 0) < d.val:
                        eng.wait_ge(d.sem, d.val)
                        waited[k] = d.val
                ins = o.fn(eng)
                if o.is_dma:
                    ins.then_inc(o.sem, 16)
                elif o.needs_inc:
                    ins.then_inc(o.sem, 1)
            for (s, v) in finals:
                k = id(s)
                if waited.get(k, 0) < v:
                    eng.wait_ge(s, v)
                    waited[k] = v

        with nc.Block() as block:
            @block.tensor
            def _(eng):
                emit("pe", eng)

            @block.scalar
            def _(eng):
                emit("act", eng)

            @block.vector
            def _(eng):
                emit("dve", eng)

            @block.gpsimd
            def _(eng):
                emit("pool", eng)

            @block.sync
            def _(eng):
                emit("sp", eng)


class T:
    def __init__(self, t, name, n=1):
        self.t = t
        self.bs = [Buf(f"{name}_{i}") for i in range(n)]
        self.b = self.bs[0]


def build(phases=3, dbg=False):
    nc = bass.Bass("TRN2", target_bir_lowering=False)

    def din(name, shape):
        return nc.dram_tensor(name, shape, F32, kind="ExternalInput").ap()

    xT = din("xT", [D, S])
    gam = din("gam", [128, 24])
    w_in0 = din("w_in0", [D, 3072])
    cw = din("cw", [128, 48])
    vec = din("vec", [128, 48])
    wa = din("wa", [12, 128, 128])
    wx = din("wx", [12, 128, 128])
    w_out0 = din("w_out0", [W, D])
    w_in1 = din("w_in1", [D, 4112])
    bfv = din("bfv", [16, 1])
    w_out1 = din("w_out1", [D, D])
    ident_d = din("ident", [128, 128])
    tri_d = din("tri", [128, 128])
    tokmask_d = din("tokmask", [128, 512])
    lsub_d = din("lsub", [16, 512])
    out = nc.dram_tensor("out", [D, 2048], F32, kind="ExternalOutput").ap()
    skind = "ExternalOutput" if dbg else "Internal"
    ys = nc.dram_tensor("ys", [W, S], BF16, kind=skind).ap()
    x1o = nc.dram_tensor("x1o", [D, 2048], F32, kind=skind).ap()
    ks = nc.dram_tensor("ks", [D, S], BF16, kind=skind).ap()
    vs = nc.dram_tensor("vs", [16, 128, 64, 128], BF16, kind=skind).ap()
    cpos = nc.dram_tensor("cpos", [16, 3, S], BF16, kind=skind).ap()
    cneg = nc.dram_tensor("cneg", [16, 3, S], BF16, kind=skind).ap()

    with contextlib.ExitStack() as gst:
        S_ = Sched(nc, gst)

        with contextlib.ExitStack() as st:
            def sb(name, shape, dt, n=1):
                return T(st.enter_context(nc.sbuf_tensor(name, shape, dt)), name, n)

            TT = 256
            NT1 = S // TT
            NCH = 12
            NI = NT1 * NCH
            w0 = sb("w0", [128, 8, 3072], BF16)
            wab = sb("wab", [128, 12, 128], BF16)
            wxb = sb("wxb", [128, 12, 128], BF16)
            dg = sb("dg", [128, 48, 128], BF16)
            gam_sb = sb("gam_sb", [128, 24], F32)
            vec_sb = sb("vec_sb", [128, 48], F32)
            onesb = sb("onesb", [128, 128], BF16)
            hc = sb("hc", [128, 12], F32)
            hb = sb("hb", [128, 24], F32)
            ctmp = sb("ctmp", [128, 12], F32)
            tokm = sb("tokm", [128, 512], BF16)
            X = [sb(f"X{i}", [128, 8, TT], F32) for i in range(2)]
            sq = sb("sq", [128, 8, TT], BF16)
            xbf = [sb(f"xbf{i}", [128, 8, TT], BF16, 8) for i in range(2)]
            rt = sb("rt", [128, TT], F32)
            rstd = [sb(f"rstd{i}", [128, TT], F32) for i in range(2)]
            rstdh = [sb(f"rstdh{i}", [128, TT], F32) for i in range(2)]
            xb = sb("xb", [128, 12, TT + 3], BF16, 12)
            NXC, NXCB, NTHR, NTHI, NGH, NTHG = 5, 3, 2, 2, 3, 2
            xcr = [sb(f"xcr{i}", [128, TT], F32) for i in range(NXC)]
            xcbr = [sb(f"xcbr{i}", [128, TT], BF16) for i in range(NXCB)]
            thr = [sb(f"thr{i}", [128, TT], F32) for i in range(NTHR)]
            thi = [sb(f"thi{i}", [128, TT], F32) for i in range(NTHI)]
            ghr = [sb(f"ghr{i}", [128, TT], F32) for i in range(NGH)]
            thg = [sb(f"thg{i}", [128, TT], F32) for i in range(NTHG)]
            aa = [sb(f"aa{i}", [128, 12, TT], F32, 12) for i in range(2)]
            u1 = [sb(f"u1{i}", [128, 12, TT], F32, 12) for i in range(2)]
            sg = [sb(f"sg{i}", [128, 12, TT], BF16, 12) for i in range(2)]
            tmp = sb("tmp", [128, 12, TT], F32, 12)
            hprev = sb("hprev", [128, 12], F32, 12)
            yb = [sb(f"yb{i}", [128, 12, TT], BF16, 12) for i in range(2)]
            banks = [st.enter_context(nc.psum_tensor(f"pbA{i}", [128, 512], F32)) for i in range(8)]
            bank_bufs = [Buf(f"pbA{i}") for i in range(8)]

            class PSl:
                def __init__(self, bank, half):
                    self.bank = banks[bank]
                    self.c0 = half * TT
                    self.b = bank_bufs[bank]

                def ap(self):
                    return self.bank[:, self.c0:self.c0 + TT]

            NUG = 3
            psUs = [PSl(i, 0) for i in range(NUG)]
            psGs = [PSl(i, 1) for i in range(NUG)]
            psCs = [PSl(3, 0), PSl(4, 0)]
            psRs = [PSl(5, 0), PSl(6, 0)]
            psIs = [PSl(5, 1), PSl(6, 1)]
            psN = PSl(7, 0)

            with contextlib.ExitStack() as st0:
                identf = T(st0.enter_context(nc.sbuf_tensor("identf", [128, 128], F32)), "identf")
                cw_sb = T(st0.enter_context(nc.sbuf_tensor("cw_sb", [128, 48], F32)), "cw_sb")
                for (dst, src) in ((gam_sb, gam), (cw_sb, cw), (vec_sb, vec), (identf, ident_d)):
                    S_.dma(lambda e, d=dst, s=src: e.dma_start(out=d.t[:], in_=s[:, :]), writes=[dst.b], sem_buf=dst.b)
                for i in range(48):
                    S_.op("pool", lambda e, i=i: e.tensor_scalar(out=dg.t[:, i, :], in0=identf.t[:], scalar1=cw_sb.t[:, i:i + 1],
                                                                 scalar2=None, op0=ALU.mult),
                          reads=[identf.b, cw_sb.b], writes=[dg.b])
                S_.flush()
            S_.dma(lambda e: e.dma_start(out=tokm.t[:], in_=tokmask_d[:, :]), writes=[tokm.b], sem_buf=tokm.b, queue="pool")
            w0v = w_in0.rearrange("(k p) n -> p k n", p=128)
            for k in range(8):
                S_.dma(lambda e, k=k: e.dma_start(out=w0.t[:, k, :], in_=w0v[:, k, :]), writes=[w0.b], sem_buf=w0.b, queue="pool")
            S_.dma(lambda e: e.dma_start(out=wab.t[:], in_=wa.rearrange("n c d -> c n d")), writes=[wab.b], sem_buf=wab.b, queue="pool")
            S_.dma(lambda e: e.dma_start(out=wxb.t[:], in_=wx.rearrange("n c d -> c n d")), writes=[wxb.b], sem_buf=wxb.b, queue="pool")
            S_.op("pool", lambda e: e.memset(onesb.t[:], 1.0), writes=[onesb.b])
            S_.op("pool", lambda e: e.memset(hprev.t[:], 0.0), writes=hprev.bs)
            S_.op("pool", lambda e: e.memset(xb.t[:], 0.0), writes=xb.bs)
            S_.op("act", lambda e: e.activation(out=ctmp.t[:], in_=vec_sb.t[:, 36:48], func=AF.Exp, scale=-1.0),
                  reads=[vec_sb.b], writes=[ctmp.b])
            S_.op("act", lambda e: e.activation(out=ctmp.t[:], in_=ctmp.t[:], func=AF.Ln, bias=1.0),
                  reads=[ctmp.b], writes=[ctmp.b])
            S_.op("dve", lambda e: e.tensor_scalar(out=hc.t[:], in0=ctmp.t[:], scalar1=-4.0, scalar2=None, op0=ALU.mult),
                  reads=[ctmp.b], writes=[hc.b])
            S_.op("dve", lambda e: e.tensor_scalar(out=hb.t[:], in0=vec_sb.t[:, 12:36], scalar1=0.5, scalar2=None, op0=ALU.mult),
                  reads=[vec_sb.b], writes=[hb.b])

            xTv = xT.rearrange("(k p) t -> p k t", p=128)
            ysv = ys.rearrange("(c p) t -> p c t", p=128)

            def load_x(i):
                Xc = X[i % 2]
                S_.dma(lambda e, t0=i * TT, Xc=Xc: e.dma_start(out=Xc.t[:], in_=xTv[:, :, t0:t0 + TT]), writes=[Xc.b], sem_buf=Xc.b)

            def casts(i):
                Xc, xbc = X[i % 2], xbf[i % 2]
                for k in range(8):
                    S_.op("pool", lambda e, k=k, Xc=Xc, xbc=xbc: e.tensor_scalar(out=xbc.t[:, k, :], in0=Xc.t[:, k, :],
                                                                                 scalar1=gam_sb.t[:, k:k + 1], scalar2=None, op0=ALU.mult),
                          reads=[Xc.b, gam_sb.b], writes=[xbc.bs[k]])

            def norm_a(i):
                Xc = X[i % 2]
                S_.op("pool", lambda e, Xc=Xc: e.tensor_tensor(out=sq.t[:], in0=Xc.t[:], in1=Xc.t[:], op=ALU.mult),
                      reads=[Xc.b], writes=[sq.b])

                def mmN(e):
                    for k in range(8):
                        ins = e.matmul(psN.ap(), lhsT=onesb.t[:], rhs=sq.t[:, k, :], start=(k == 0), stop=(k == 7))
                    return ins
                S_.op("pe", mmN, reads=[onesb.b, sq.b], writes=[psN.b])
                S_.op("dve", lambda e: e.tensor_scalar(out=rt.t[:], in0=psN.ap(), scalar1=1.0 / D, scalar2=EPS,
                                                       op0=ALU.mult, op1=ALU.add), reads=[psN.b], writes=[rt.b])

            def norm_sqrt():
                S_.op("act", lambda e: e.activation(out=rt.t[:], in_=rt.t[:], func=AF.Sqrt), reads=[rt.b], writes=[rt.b])

            def norm_b(i):
                r_, rh_ = rstd[i % 2], rstdh[i % 2]
                S_.op("dve", lambda e, r_=r_: e.reciprocal(out=r_.t[:], in_=rt.t[:]), reads=[rt.b], writes=[r_.b])
                S_.op("dve", lambda e, r_=r_, rh_=rh_: e.tensor_scalar(out=rh_.t[:], in0=r_.t[:], scalar1=0.5, scalar2=None, op0=ALU.mult),
                      reads=[r_.b], writes=[rh_.b])

            def op_UG(it, m):
                xbc = xbf[it % 2]
                n = it * NCH + m
                pu, pg = psUs[n % NUG], psGs[n % NUG]

                def mmU(e):
                    for k in range(8):
                        ins = e.matmul(pu.ap(), lhsT=w0.t[:, k, m * 128:(m + 1) * 128], rhs=xbc.t[:, k, :], start=(k == 0), stop=(k == 7))
                    return ins
                S_.op("pe", mmU, reads=[w0.b] + xbc.bs, writes=[pu.b])

                def mmG(e):
                    for k in range(8):
                        ins = e.matmul(pg.ap(), lhsT=w0.t[:, k, W + m * 128:W + (m + 1) * 128], rhs=xbc.t[:, k, :], start=(k == 0), stop=(k == 7))
                    return ins
                S_.op("pe", mmG, reads=[w0.b] + xbc.bs, writes=[pg.b])

            def op_EV(it, m):
                n = it * NCH + m
                pu, pg = psUs[n % NUG], psGs[n % NUG]
                r_, rh_ = rstd[it % 2], rstdh[it % 2]
                g_ = ghr[n % NGH]
                S_.op("dve", lambda e: e.tensor_tensor(out=xb.t[:, m, 3:3 + TT], in0=pu.ap(), in1=r_.t[:], op=ALU.mult),
                      reads=[pu.b, r_.b], writes=[xb.bs[m]])
                S_.op("dve", lambda e: e.tensor_tensor(out=g_.t[:], in0=pg.ap(), in1=rh_.t[:], op=ALU.mult),
                      reads=[pg.b, rh_.b], writes=[g_.b])

            def op_C(it, m):
                n = it * NCH + m
                pc = psCs[n % 2]

                def mmC(e):
                    for k in range(4):
                        ins = e.matmul(pc.ap(), lhsT=dg.t[:, k * 12 + m, :], rhs=xb.t[:, m, k:k + TT], start=(k == 0), stop=(k == 3))
                    return ins
                S_.op("pe", mmC, reads=[dg.b, xb.bs[m]], writes=[pc.b])

            def op_THG(it, m):
                n = it * NCH + m
                g_, tg_ = ghr[n % NGH], thg[n % NTHG]
                S_.op("act", lambda e: e.activation(out=tg_.t[:], in_=g_.t[:], func=AF.Tanh), reads=[g_.b], writes=[tg_.b])

            def op_SG(it, m):
                n = it * NCH + m
                g_, tg_, sgc = ghr[n % NGH], thg[n % NTHG], sg[it % 2]
                S_.op("dve", lambda e: e.scalar_tensor_tensor(out=sgc.t[:, m, :], in0=tg_.t[:], scalar=1.0, in1=g_.t[:],
                                                              op0=ALU.add, op1=ALU.mult),
                      reads=[g_.b, tg_.b], writes=[sgc.bs[m]])

            def op_XC(it, m):
                n = it * NCH + m
                pc, xc_ = psCs[n % 2], xcr[n % NXC]
                S_.op("act", lambda e: e.activation(out=xc_.t[:], in_=pc.ap(), func=AF.Identity, bias=vec_sb.t[:, m:m + 1]),
                      reads=[pc.b, vec_sb.b], writes=[xc_.b])

            def op_HALO(it, m):
                S_.op("pool", lambda e: e.tensor_copy(out=xb.t[:, m, 0:3], in_=xb.t[:, m, TT:TT + 3]),
                      reads=[xb.bs[m]], writes=[xb.bs[m]])

            def op_XCB(it, m):
                n = it * NCH + m
                pc, xcb_ = psCs[n % 2], xcbr[n % NXCB]
                S_.op("dve", lambda e: e.tensor_scalar(out=xcb_.t[:], in0=pc.ap(), scalar1=vec_sb.t[:, m:m + 1], scalar2=None, op0=ALU.add),
                      reads=[vec_sb.b], writes=[xcb_.b, pc.b])

            def op_RI(it, m):
                n = it * NCH + m
                pr, pi, xcb_ = psRs[n % 2], psIs[n % 2], xcbr[n % NXCB]
                S_.op("pe", lambda e: e.matmul(pr.ap(), lhsT=wab.t[:, m, :], rhs=xcb_.t[:], start=True, stop=True),
                      reads=[wab.b, xcb_.b], writes=[pr.b])
                S_.op("pe", lambda e: e.matmul(pi.ap(), lhsT=wxb.t[:, m, :], rhs=xcb_.t[:], start=True, stop=True),
                      reads=[wxb.b, xcb_.b], writes=[pi.b])

            def op_TH(it, m):
                n = it * NCH + m
                pr, pi, tr_, ti_ = psRs[n % 2], psIs[n % 2], thr[n % NTHR], thi[n % NTHI]
                S_.op("act", lambda e: e.activation(out=tr_.t[:], in_=pr.ap(), func=AF.Tanh, bias=hb.t[:, m:m + 1], scale=0.5),
                      reads=[pr.b, hb.b], writes=[tr_.b])
                S_.op("act", lambda e: e.activation(out=ti_.t[:], in_=pi.ap(), func=AF.Tanh, bias=hb.t[:, 12 + m:13 + m], scale=0.5),
                      reads=[pi.b, hb.b], writes=[ti_.b])

            def op_AA(it, m):
                n = it * NCH + m
                tr_, aac = thr[n % NTHR], aa[it % 2]
                S_.op("act", lambda e: e.activation(out=aac.t[:, m, :], in_=tr_.t[:], func=AF.Exp, scale=hc.t[:, m:m + 1], bias=hc.t[:, m:m + 1]),
                      reads=[tr_.b, hc.b], writes=[aac.bs[m]])

            def op_U1(it, m):
                n = it * NCH + m
                xc_, ti_, u1c = xcr[n % NXC], thi[n % NTHI], u1[it % 2]
                S_.op("dve", lambda e: e.scalar_tensor_tensor(out=u1c.t[:, m, :], in0=ti_.t[:], scalar=1.0, in1=xc_.t[:],
                                                              op0=ALU.add, op1=ALU.mult),
                      reads=[xc_.b, ti_.b], writes=[u1c.bs[m]])

            def op_MID(it, m):
                u1p, aap, sgp, ybc = u1[it % 2], aa[it % 2], sg[it % 2], yb[it % 2]
                t0p = it * TT
                S_.op("pool", lambda e: e.tensor_tensor(out=u1p.t[:, m, :], in0=u1p.t[:, m, :], in1=tmp.t[:, m, :], op=ALU.mult),
                      reads=[u1p.bs[m], tmp.bs[m]], writes=[u1p.bs[m]])
                if t0p < 512:
                    S_.op("pool", lambda e: e.tensor_tensor(out=u1p.t[:, m, :], in0=u1p.t[:, m, :], in1=tokm.t[:, t0p:t0p + TT], op=ALU.mult),
                          reads=[u1p.bs[m], tokm.b], writes=[u1p.bs[m]])
                S_.op("dve", lambda e: e.tensor_tensor_scan(out=tmp.t[:, m, :], data0=aap.t[:, m, :], data1=u1p.t[:, m, :],
                                                            initial=hprev.t[:, m:m + 1], op0=ALU.mult, op1=ALU.add),
                      reads=[aap.bs[m], u1p.bs[m], hprev.bs[m], tmp.bs[m]], writes=[tmp.bs[m]])
                S_.op("pool", lambda e: e.tensor_copy(out=hprev.t[:, m:m + 1], in_=tmp.t[:, m, TT - 1:TT]),
                      reads=[tmp.bs[m]], writes=[hprev.bs[m]])
                S_.op("pool", lambda e: e.tensor_tensor(out=ybc.t[:, m, :], in0=tmp.t[:, m, :], in1=sgp.t[:, m, :], op=ALU.mult),
                      reads=[tmp.bs[m], sgp.bs[m]], writes=[ybc.bs[m]])
                if m == NCH - 1:
                    S_.dma(lambda e: e.dma_start(out=ysv[:, :, t0p:t0p + TT], in_=ybc.t[:]), reads=ybc.bs, sem_buf=ybc.bs[0])

            def batch(it):
                aap = aa[it % 2]
                S_.op("act", lambda e: e.activation(out=tmp.t[:], in_=aap.t[:], func=AF.Square), reads=aap.bs, writes=tmp.bs)
                S_.op("act", lambda e: e.activation(out=tmp.t[:], in_=tmp.t[:], func=AF.Sqrt, scale=-0.25, bias=0.25),
                      reads=tmp.bs, writes=tmp.bs)

            LAGS = [(0, op_UG), (1, op_EV), (2, op_C), (2, op_THG), (3, op_SG), (3, op_XC), (3, op_HALO), (3, op_XCB),
                    (4, op_RI), (5, op_TH), (6, op_AA), (6, op_U1), (19, op_MID)]
            load_x(0)
            load_x(1)
            casts(0)
            norm_a(0)
            norm_sqrt()
            norm_b(0)
            load_x(2)
            for g in range(NI + 20):
                for (L, fn) in LAGS:
                    n = g - L
                    if 0 <= n < NI:
                        fn(n // NCH, n % NCH)
                t, r = divmod(g, NCH)
                if r == 4 and t + 1 < NT1:
                    norm_a(t + 1)
                if r == 6:
                    if t + 1 < NT1:
                        norm_sqrt()
                    if 0 <= t - 1 < NT1:
                        batch(t - 1)
                    if t + 1 < NT1:
                        norm_b(t + 1)
                if r == 9 and t + 1 < NT1:
                    casts(t + 1)
                    if t + 3 < NT1:
                        load_x(t + 3)
            S_.flush()

        if phases < 2:
            return nc
        with contextlib.ExitStack() as st:
            def sb(name, shape, dt, n=1):
                return T(st.enter_context(nc.sbuf_tensor(name, shape, dt)), name, n)

            def ps(name):
                return T(st.enter_context(nc.psum_tensor(name, [128, 512], F32)), name)

            TT = 512
            NT2 = S // TT
            wo0 = sb("wo0", [128, 12, 1024], BF16)
            wk = sb("wk", [128, 8, 1024], BF16)
            wv = sb("wv", [128, 8, 1024], BF16)
            wf = sb("wf", [128, 8, 16], BF16)
            gam_sb = sb("gam_sb2", [128, 24], F32)
            onesb = sb("onesb2", [128, 128], BF16)
            ones16 = sb("ones16", [16, 512], F32)
            nbf = sb("nbf", [16, 1], F32)
            lsub = sb("lsub_sb", [16, 512], F32)
            X = [sb(f"X2_{i}", [128, 8, TT], F32, 8) for i in range(2)]
            Y = [sb(f"Y2_{i}", [128, 12, TT], BF16) for i in range(2)]
            sq = [sb(f"sq2_{i}", [128, 8, TT], BF16, 8) for i in range(2)]
            xbf = [sb(f"xbf2_{i}", [128, 8, TT], BF16, 8) for i in range(2)]
            rt = sb("rt2", [128, TT], F32)
            rstd = [sb(f"rstd2_{i}", [128, TT], F32) for i in range(2)]
            rtT = sb("rtT", [128, 4], F32)
            rstdT = [sb(f"rstdT{i}", [128, 4], F32) for i in range(2)]
            kst = sb("kst", [128, 8, TT], BF16)
            vst = sb("vst", [128, 8, 2, 4, 128], BF16)
            zf = sb("zf", [16, 512], F32)
            cc = sb("cc", [16, 512], F32)
            r1 = sb("r1", [16, 512], F32)
            cprev = sb("cprev", [16, 1], F32)
            Pp = sb("Pp", [16, 3, 512], BF16)
            Pn = sb("Pn", [16, 3, 512], BF16)
            psO = [ps("psO0"), ps("psO1")]
            psN = ps("psN2")
            psT = ps("psT2")
            psK = [ps("psK0"), ps("psK1")]
            psV = [ps("psV0"), ps("psV1")]

            w1v = w_in1.rearrange("(k p) n -> p k n", p=128)
            wo0v = w_out0.rearrange("(k p) n -> p k n", p=128)
            for k in range(12):
                S_.dma(lambda e, k=k: e.dma_start(out=wo0.t[:, k, :], in_=wo0v[:, k, :]), writes=[wo0.b], sem_buf=wo0.b, queue="pool")
            for k in range(8):
                S_.dma(lambda e, k=k: e.dma_start(out=wk.t[:, k, :], in_=w1v[:, k, 1024:2048]), writes=[wk.b], sem_buf=wk.b, queue="pool")
                S_.dma(lambda e, k=k: e.dma_start(out=wv.t[:, k, :], in_=w1v[:, k, 2048:3072]), writes=[wv.b], sem_buf=wv.b, queue="pool")
            S_.dma(lambda e: e.dma_start(out=wf.t[:], in_=w1v[:, :, 4096:4112]), writes=[wf.b], sem_buf=wf.b, queue="pool")
            S_.dma(lambda e: e.dma_start(out=gam_sb.t[:], in_=gam[:, :]), writes=[gam_sb.b], sem_buf=gam_sb.b)
            S_.dma(lambda e: e.dma_start(out=nbf.t[:], in_=bfv[:, :]), writes=[nbf.b], sem_buf=nbf.b)
            S_.dma(lambda e: e.dma_start(out=lsub.t[:], in_=lsub_d[:, :]), writes=[lsub.b], sem_buf=lsub.b)
            S_.op("dve", lambda e: e.tensor_scalar(out=nbf.t[:], in0=nbf.t[:], scalar1=-1.0, scalar2=None, op0=ALU.mult),
                  reads=[nbf.b], writes=[nbf.b])
            S_.op("pool", lambda e: e.memset(onesb.t[:], 1.0), writes=[onesb.b])
            S_.op("pool", lambda e: e.memset(ones16.t[:], 1.0), writes=[ones16.b])
            S_.op("pool", lambda e: e.memset(cprev.t[:], 0.0), writes=[cprev.b])
            S_.op("pool", lambda e: e.memset(vst.t[:], 1.0), writes=[vst.b])

            xTv = xT.rearrange("(k p) t -> p k t", p=128)
            ysv = ys.rearrange("(c p) t -> p c t", p=128)
            x1ov = x1o.rearrange("(k p) t -> p k t", p=128)
            ksv = ks.rearrange("(m p) t -> p m t", p=128)
            vsv = vs.rearrange("(g e) p kb c -> p g e kb c", e=2)

            def load2(i):
                Xc, Yc = X[i % 2], Y[i % 2]
                S_.dma(lambda e, t0=i * TT, Xc=Xc: e.dma_start(out=Xc.t[:], in_=xTv[:, :, t0:t0 + TT]), writes=Xc.bs, sem_buf=Xc.bs[0])
                S_.dma(lambda e, t0=i * TT, Yc=Yc: e.dma_start(out=Yc.t[:], in_=ysv[:, :, t0:t0 + TT]), writes=[Yc.b], sem_buf=Yc.b)

            def stage_o(i):
                p_ = i % 2
                Xc, Yc, sqc, xbc = X[p_], Y[p_], sq[p_], xbf[p_]
                for mo in range(8):
                    po = psO[mo % 2]

                    def mmO(e, mo=mo, po=po, Yc=Yc):
                        for c in range(12):
                            ins = e.matmul(po.t[:, :], lhsT=wo0.t[:, c, mo * 128:(mo + 1) * 128], rhs=Yc.t[:, c, :],
                                           start=(c == 0), stop=(c == 11))
                        return ins
                    S_.op("pe", mmO, reads=[wo0.b, Yc.b], writes=[po.b])
                    S_.op("dve", lambda e, mo=mo, po=po, Xc=Xc: e.tensor_tensor(out=Xc.t[:, mo, :], in0=po.t[:, :], in1=Xc.t[:, mo, :], op=ALU.add),
                          reads=[po.b, Xc.bs[mo]], writes=[Xc.bs[mo]])
                    S_.op("pool", lambda e, mo=mo, Xc=Xc, sqc=sqc: e.tensor_tensor(out=sqc.t[:, mo, :], in0=Xc.t[:, mo, :], in1=Xc.t[:, mo, :], op=ALU.mult),
                          reads=[Xc.bs[mo]], writes=[sqc.bs[mo]])
                    S_.op("act", lambda e, mo=mo, Xc=Xc, xbc=xbc: e.activation(out=xbc.t[:, mo, :], in_=Xc.t[:, mo, :], func=AF.Copy,
                                                                              scale=gam_sb.t[:, 8 + mo:9 + mo]),
                          reads=[Xc.bs[mo], gam_sb.b], writes=[xbc.bs[mo]])
                S_.dma(lambda e, i=i, Xc=Xc: e.dma_start(out=x1ov[:, :, i * 128:(i + 1) * 128], in_=Xc.t[:, :, 384:512]),
                       reads=Xc.bs, sem_buf=Xc.bs[1])

                def mmN(e, sqc=sqc):
                    for k in range(8):
                        ins = e.matmul(psN.t[:, :], lhsT=onesb.t[:], rhs=sqc.t[:, k, :], start=(k == 0), stop=(k == 7))
                    return ins
                S_.op("pe", mmN, reads=[onesb.b] + sqc.bs, writes=[psN.b])

                def mmT(e, sqc=sqc):
                    for j in range(4):
                        for k in range(8):
                            ins = e.matmul(psT.t[:, j:j + 1], lhsT=sqc.t[:, k, j * 128:(j + 1) * 128], rhs=onesb.t[:, 0:1],
                                           start=(k == 0), stop=(k == 7))
                    return ins
                S_.op("pe", mmT, reads=[onesb.b] + sqc.bs, writes=[psT.b])
                S_.op("dve", lambda e: e.tensor_scalar(out=rt.t[:], in0=psN.t[:, :], scalar1=1.0 / D, scalar2=EPS,
                                                       op0=ALU.mult, op1=ALU.add), reads=[psN.b], writes=[rt.b])
                S_.op("dve", lambda e: e.tensor_scalar(out=rtT.t[:], in0=psT.t[:, 0:4], scalar1=1.0 / D, scalar2=EPS,
                                                       op0=ALU.mult, op1=ALU.add), reads=[psT.b], writes=[rtT.b])
                S_.op("act", lambda e: e.activation(out=rt.t[:], in_=rt.t[:], func=AF.Sqrt), reads=[rt.b], writes=[rt.b])
                S_.op("act", lambda e: e.activation(out=rtT.t[:], in_=rtT.t[:], func=AF.Sqrt), reads=[rtT.b], writes=[rtT.b])
                S_.op("dve", lambda e, r_=rstd[p_]: e.reciprocal(out=r_.t[:], in_=rt.t[:]), reads=[rt.b], writes=[rstd[p_].b])
                S_.op("dve", lambda e, r_=rstdT[p_]: e.reciprocal(out=r_.t[:], in_=rtT.t[:]), reads=[rtT.b], writes=[rstdT[p_].b])

            def stage_kv(i):
                p_ = i % 2
                t0 = i * TT
                xbc, r_, rT_ = xbf[p_], rstd[p_], rstdT[p_]
                for m in range(8):
                    pk = psK[m % 2]

                    def mmK(e, m=m, pk=pk):
                        for k in range(8):
                            ins = e.matmul(pk.t[:, :], lhsT=wk.t[:, k, m * 128:(m + 1) * 128], rhs=xbc.t[:, k, :],
                                           start=(k == 0), stop=(k == 7))
                        return ins
                    S_.op("pe", mmK, reads=[wk.b] + xbc.bs, writes=[pk.b])
                    S_.op("dve", lambda e, m=m, pk=pk: e.tensor_tensor(out=kst.t[:, m, :], in0=pk.t[:, :], in1=r_.t[:], op=ALU.mult),
                          reads=[pk.b, r_.b], writes=[kst.b])
                S_.dma(lambda e, t0=t0: e.dma_start(out=ksv[:, :, t0:t0 + TT], in_=kst.t[:]), reads=[kst.b], sem_buf=kst.b)
                for j in range(4):
                    for half in range(2):
                        pv = psV[(j * 2 + half) % 2]

                        def mmV(e, j=j, half=half, pv=pv):
                            for k in range(8):
                                ins = e.matmul(pv.t[:, :], lhsT=xbc.t[:, k, j * 128:(j + 1) * 128],
                                               rhs=wv.t[:, k, half * 512:(half + 1) * 512], start=(k == 0), stop=(k == 7))
                            return ins
                        S_.op("pe", mmV, reads=[wv.b] + xbc.bs, writes=[pv.b])
                        pvv = pv.t[:, :].rearrange("p (g e d) -> p g e d", g=4, e=2, d=64)
                        S_.op("act", lambda e, j=j, half=half, pvv=pvv: e.activation(
                            out=vst.t[:, half * 4:(half + 1) * 4, 0, j, 0:64], in_=pvv[:, :, 0, :], func=AF.Copy,
                            scale=rT_.t[:, j:j + 1]), reads=[pv.b, rT_.b], writes=[vst.b])
                        S_.op("act", lambda e, j=j, half=half, pvv=pvv: e.activation(
                            out=vst.t[:, half * 4:(half + 1) * 4, 1, j, 64:128], in_=pvv[:, :, 1, :], func=AF.Copy,
                            scale=rT_.t[:, j:j + 1]), reads=[pv.b, rT_.b], writes=[vst.b])
                S_.dma(lambda e, i=i: e.dma_start(out=vsv[:, :, :, 4 * i:4 * i + 4, :], in_=vst.t[:]), reads=[vst.b], sem_buf=vst.b)

                def mmF(e):
                    for k in range(8):
                        ins = e.matmul(psT.t[0:16, :], lhsT=wf.t[:, k, :], rhs=xbc.t[:, k, :], start=(k == 0), stop=(k == 7))
                    return ins
                S_.op("pe", mmF, reads=[wf.b] + xbc.bs, writes=[psT.b])
                S_.op("dve", lambda e: e.tensor_tensor(out=zf.t[:], in0=psT.t[0:16, :], in1=r_.t[0:16, :], op=ALU.mult),
                      reads=[psT.b, r_.b], writes=[zf.b])
                S_.op("act", lambda e: e.activation(out=zf.t[:], in_=zf.t[:], func=AF.Exp, scale=-1.0, bias=nbf.t[:, 0:1]),
                      reads=[zf.b, nbf.b], writes=[zf.b])
                S_.op("act", lambda e: e.activation(out=zf.t[:], in_=zf.t[:], func=AF.Ln, bias=1.0), reads=[zf.b], writes=[zf.b])
                if i == 0:
                    S_.op("dve", lambda e: e.tensor_tensor(out=zf.t[:], in0=zf.t[:], in1=lsub.t[:], op=ALU.add),
                          reads=[zf.b, lsub.b], writes=[zf.b])
                S_.op("dve", lambda e: e.tensor_tensor_scan(out=cc.t[:], data0=ones16.t[:], data1=zf.t[:], initial=cprev.t[:, 0:1],
                                                            op0=ALU.mult, op1=ALU.subtract),
                      reads=[ones16.b, zf.b, cprev.b], writes=[cc.b])
                S_.op("dve", lambda e: e.tensor_copy(out=cprev.t[:], in_=cc.t[:, TT - 1:TT]), reads=[cc.b], writes=[cprev.b])
                S_.op("dve", lambda e: e.tensor_copy(out=Pp.t[:, 0, :], in_=cc.t[:]), reads=[cc.b], writes=[Pp.b])
                S_.op("dve", lambda e: e.tensor_tensor(out=r1.t[:], in0=cc.t[:], in1=Pp.t[:, 0, :], op=ALU.subtract),
                      reads=[cc.b, Pp.b], writes=[r1.b])
                S_.op("dve", lambda e: e.tensor_copy(out=Pp.t[:, 1, :], in_=r1.t[:]), reads=[r1.b], writes=[Pp.b])
                S_.op("dve", lambda e: e.tensor_tensor(out=r1.t[:], in0=r1.t[:], in1=Pp.t[:, 1, :], op=ALU.subtract),
                      reads=[r1.b, Pp.b], writes=[r1.b])
                S_.op("dve", lambda e: e.tensor_copy(out=Pp.t[:, 2, :], in_=r1.t[:]), reads=[r1.b], writes=[Pp.b])
                S_.op("dve", lambda e: e.tensor_scalar(out=Pn.t[:], in0=Pp.t[:], scalar1=-1.0, scalar2=None, op0=ALU.mult),
                      reads=[Pp.b], writes=[Pn.b])
                S_.dma(lambda e, t0=t0: e.dma_start(out=cpos[:, :, t0:t0 + TT], in_=Pp.t[:]), reads=[Pp.b], sem_buf=Pp.b)
                S_.dma(lambda e, t0=t0: e.dma_start(out=cneg[:, :, t0:t0 + TT], in_=Pn.t[:]), reads=[Pn.b], sem_buf=Pn.b)

            load2(0)
            for it in range(NT2 + 1):
                if it + 1 < NT2:
                    load2(it + 1)
                if it < NT2:
                    stage_o(it)
                if it >= 1:
                    stage_kv(it - 1)
            S_.flush()

        if phases < 3:
            return nc
        with contextlib.ExitStack() as st:
            def sb(name, shape, dt):
                return T(st.enter_context(nc.sbuf_tensor(name, shape, dt)), name)

            def ps(name):
                return T(st.enter_context(nc.psum_tensor(name, [128, 512], F32)), name)

            TT = 512
            wq = sb("wq", [128, 8, 1024], BF16)
            wg = sb("wg", [128, 8, 1024], BF16)
            wo1 = sb("wo1", [128, 8, 1024], BF16)
            gam_sb = sb("gam_sb3", [128, 24], F32)
            onesb = sb("onesb3", [128, 128], BF16)
            identb = sb("identb", [128, 128], BF16)
            trib = sb("trib", [128, 128], BF16)
            X = sb("X3", [128, 8, TT], F32)
            sq = sb("sq3", [128, 8, TT], BF16)
            xbf = sb("xbf3", [128, 8, TT], BF16)
            rt = sb("rt3", [128, TT], F32)
            rstd = sb("rstd3", [128, TT], F32)
            rq = sb("rq3", [128, TT], F32)
            tq = [sb(f"tq{i}", [128, TT], F32) for i in range(2)]
            qa = sb("qa", [70, 16, TT], BF16)
            sg = sb("sg3", [128, 8, TT], BF16)
            yh = sb("yh", [128, 8, TT], BF16)
            ka = [sb(f"ka{i}", [70, S], BF16) for i in range(2)]
            va = [sb(f"va{i}", [128, 64, 128], BF16) for i in range(2)]
            pt = [sb(f"pt{i}", [128, TT], BF16) for i in range(5)]
            rl = sb("rl", [128, TT], F32)
            on = sb("on", [128, TT], F32)
            psN = ps("psN3")
            psS = [ps(f"psS{i}") for i in range(5)]
            psQ = [psS[0], psS[1]]
            psOo = [ps("psOb0"), ps("psOb1")]

            w1v = w_in1.rearrange("(k p) n -> p k n", p=128)
            wo1v = w_out1.rearrange("(k p) n -> p k n", p=128)
            for k in range(8):
                S_.dma(lambda e, k=k: e.dma_start(out=wq.t[:, k, :], in_=w1v[:, k, 0:1024]), writes=[wq.b], sem_buf=wq.b, queue="pool")
                S_.dma(lambda e, k=k: e.dma_start(out=wg.t[:, k, :], in_=w1v[:, k, 3072:4096]), writes=[wg.b], sem_buf=wg.b, queue="pool")
                S_.dma(lambda e, k=k: e.dma_start(out=wo1.t[:, k, :], in_=wo1v[:, k, :]), writes=[wo1.b], sem_buf=wo1.b, queue="pool")
            S_.dma(lambda e: e.dma_start(out=identb.t[:], in_=ident_d[:, :]), writes=[identb.b], sem_buf=identb.b, queue="pool")
            S_.dma(lambda e: e.dma_start(out=trib.t[:], in_=tri_d[:, :]), writes=[trib.b], sem_buf=trib.b, queue="pool")
            S_.dma(lambda e: e.dma_start(out=gam_sb.t[:], in_=gam[:, :]), writes=[gam_sb.b], sem_buf=gam_sb.b)
            S_.op("pool", lambda e: e.memset(onesb.t[:], 1.0), writes=[onesb.b])
            S_.op("pool", lambda e: e.memset(qa.t[64:70, :, :], 1.0), writes=[qa.b])
            for i in range(2):
                S_.op("pool", lambda e, i=i: e.memset(ka[i].t[64:70, :], 1.0), writes=[ka[i].b])

            x1ov = x1o.rearrange("(k p) t -> p k t", p=128)
            cpg = cpos.rearrange("h r (t c) -> r h t c", c=512)
            outv = out.rearrange("(k p) t -> p k t", p=128)
            cnt_s = 0
            cnt_p = 0
            xsem = [Buf(f"xsem{i}") for i in range(4)]
            qsem = [Buf(f"qsem{i}") for i in range(4)]
            for G in range(4):
                L = 2048 * (G + 1)
                nkb = 16 * (G + 1)
                S_.dma(lambda e, G=G: e.dma_start(out=X.t[:], in_=x1ov[:, :, G * 512:(G + 1) * 512]), writes=[X.b], sem_buf=X.b)
                S_.op("pool", lambda e: e.tensor_tensor(out=sq.t[:], in0=X.t[:], in1=X.t[:], op=ALU.mult), reads=[X.b], writes=[sq.b])
                for k in range(8):
                    S_.op("act", lambda e, k=k: e.activation(out=xbf.t[:, k, :], in_=X.t[:, k, :], func=AF.Copy,
                                                             scale=gam_sb.t[:, 8 + k:9 + k]),
                          reads=[X.b, gam_sb.b], writes=[xbf.b])

                def mmN(e):
                    for k in range(8):
                        ins = e.matmul(psN.t[:, :], lhsT=onesb.t[:], rhs=sq.t[:, k, :], start=(k == 0), stop=(k == 7))
                    return ins
                S_.op("pe", mmN, reads=[onesb.b, sq.b], writes=[psN.b])
                S_.op("dve", lambda e: e.tensor_scalar(out=rt.t[:], in0=psN.t[:, :], scalar1=1.0 / D, scalar2=EPS,
                                                       op0=ALU.mult, op1=ALU.add), reads=[psN.b], writes=[rt.b])
                S_.op("act", lambda e: e.activation(out=rt.t[:], in_=rt.t[:], func=AF.Sqrt), reads=[rt.b], writes=[rt.b])
                S_.op("dve", lambda e: e.reciprocal(out=rstd.t[:], in_=rt.t[:]), reads=[rt.b], writes=[rstd.b])
                S_.op("dve", lambda e: e.tensor_scalar(out=rq.t[:], in0=rstd.t[:], scalar1=0.125, scalar2=None, op0=ALU.mult),
                      reads=[rstd.b], writes=[rq.b])
                for m in range(8):
                    pq = psQ[m % 2]
                    tqm = tq[m % 2]

                    def mmQ(e, m=m, pq=pq):
                        for k in range(8):
                            ins = e.matmul(pq.t[:, :], lhsT=wq.t[:, k, m * 128:(m + 1) * 128], rhs=xbf.t[:, k, :],
                                           start=(k == 0), stop=(k == 7))
                        return ins
                    S_.op("pe", mmQ, reads=[wq.b, xbf.b], writes=[pq.b])
                    S_.op("dve", lambda e, pq=pq, tqm=tqm: e.tensor_tensor(out=tqm.t[:], in0=pq.t[:, :], in1=rq.t[:], op=ALU.mult),
                          reads=[pq.b, rq.b], writes=[tqm.b])
                    S_.op("act", lambda e, m=m, tqm=tqm: e.activation(out=qa.t[0:64, 2 * m, :], in_=tqm.t[0:64, :], func=AF.Copy),
                          reads=[tqm.b], writes=[qa.b])
                    S_.op("act", lambda e, m=m, tqm=tqm: e.activation(out=qa.t[0:64, 2 * m + 1, :], in_=tqm.t[64:128, :], func=AF.Copy),
                          reads=[tqm.b], writes=[qa.b])
                for a_ in range(4):
                    S_.dma(lambda e, G=G, a_=a_: e.dma_start(out=qa.t[67:70, :, a_ * 128:(a_ + 1) * 128],
                                                             in_=cpg[:, :, 4 * G + a_, 384:512]), writes=[qa.b], sem_buf=qsem[a_])
                for m in range(8):
                    pq = psQ[m % 2]
                    tqm = tq[m % 2]

                    def mmG(e, m=m, pq=pq):
                        for k in range(8):
                            ins = e.matmul(pq.t[:, :], lhsT=wg.t[:, k, m * 128:(m + 1) * 128], rhs=xbf.t[:, k, :],
                                           start=(k == 0), stop=(k == 7))
                        return ins
                    S_.op("pe", mmG, reads=[wg.b, xbf.b], writes=[pq.b])
                    S_.op("dve", lambda e, pq=pq, tqm=tqm: e.tensor_tensor(out=tqm.t[:], in0=pq.t[:, :], in1=rstd.t[:], op=ALU.mult),
                          reads=[pq.b, rstd.b], writes=[tqm.b])
                    S_.op("act", lambda e, m=m, tqm=tqm: e.activation(out=sg.t[:, m, :], in_=tqm.t[:], func=AF.Silu),
                          reads=[tqm.b], writes=[sg.b])
                for h in range(16):
                    kab = ka[h % 2]
                    vab = va[h % 2]
                    S_.dma(lambda e, h=h, kab=kab, L=L: e.dma_start(out=kab.t[0:64, 0:L], in_=ks[h * 64:(h + 1) * 64, 0:L]),
                           writes=[kab.b], sem_buf=kab.b)
                    S_.dma(lambda e, h=h, kab=kab, L=L: e.dma_start(out=kab.t[64:67, 0:L], in_=cneg[h, :, 0:L]),
                           writes=[kab.b], sem_buf=kab.b)
                    S_.dma(lambda e, h=h, vab=vab, nkb=nkb: e.dma_start(out=vab.t[:, 0:nkb, :], in_=vs[h, :, 0:nkb, :]),
                           writes=[vab.b], sem_buf=vab.b)
                    blocks = [(kb, 0, False) for kb in range(16 * G)]
                    for a_ in range(4):
                        for i in range(4):
                            blocks.append((16 * G + 4 * a_ + i, 128 * a_, i == 3))
                    po = psOo[h % 2]
                    nb = len(blocks)
                    LOOK = 2
                    pend = []

                    def emit_pv(item, po=po, vab=vab, nb=nb):
                        (bi, kb, c0, ptt) = item
                        S_.op("pe", lambda e, kb=kb, c0=c0, ptt=ptt, vab=vab, po=po, bi=bi, nb=nb: e.matmul(
                            po.t[:, c0:512], lhsT=vab.t[:, kb, :], rhs=ptt.t[:, c0:512], start=(bi == 0), stop=(bi == nb - 1)),
                            reads=[vab.b, ptt.b], writes=[po.b])

                    for bi, (kb, c0, diag) in enumerate(blocks):
                        pss = psS[cnt_s % len(psS)]
                        cnt_s += 1
                        ptt = pt[cnt_p % len(pt)]
                        cnt_p += 1

                        def mmS(e, kb=kb, c0=c0, diag=diag, pss=pss, kab=kab, h=h):
                            ins = e.matmul(pss.t[:, c0:512], lhsT=kab.t[0:70, kb * 128:(kb + 1) * 128], rhs=qa.t[0:70, h, c0:512],
                                           start=True, stop=not diag)
                            if diag:
                                ins = e.matmul(pss.t[:, c0:c0 + 128], lhsT=identb.t[:], rhs=trib.t[:], start=False, stop=True)
                            return ins
                        S_.op("pe", mmS, reads=[kab.b, qa.b, identb.b, trib.b], writes=[pss.b])
                        S_.op("act", lambda e, c0=c0, pss=pss, ptt=ptt: e.activation(out=ptt.t[:, c0:512], in_=pss.t[:, c0:512], func=AF.Exp),
                              reads=[pss.b], writes=[ptt.b])
                        pend.append((bi, kb, c0, ptt))
                        if len(pend) > LOOK:
                            emit_pv(pend.pop(0))
                    while pend:
                        emit_pv(pend.pop(0))
                    if h % 2 == 0:
                        lo, hi, lo2, hi2 = 0, 64, 64, 128
                    else:
                        lo, hi, lo2, hi2 = 64, 128, 0, 64
                    S_.op("dve", lambda e, po=po, lo2=lo2, hi2=hi2: e.reciprocal(out=rl.t[lo2:hi2, :], in_=po.t[lo2:hi2, :]),
                          reads=[po.b], writes=[rl.b])
                    S_.op("dve", lambda e, po=po, lo=lo, hi=hi, lo2=lo2, hi2=hi2: e.tensor_tensor(
                        out=on.t[lo:hi, :], in0=po.t[lo:hi, :], in1=rl.t[lo2:hi2, :], op=ALU.mult),
                        reads=[po.b, rl.b], writes=[on.b])
                    S_.op("pool", lambda e, h=h, lo=lo, hi=hi: e.tensor_tensor(
                        out=yh.t[lo:hi, h // 2, :], in0=on.t[lo:hi, :], in1=sg.t[lo:hi, h // 2, :], op=ALU.mult),
                        reads=[on.b, sg.b], writes=[yh.b])
                for mo in range(8):
                    pq = psQ[mo % 2]

                    def mmP(e, mo=mo, pq=pq):
                        for c in range(8):
                            ins = e.matmul(pq.t[:, :], lhsT=wo1.t[:, c, mo * 128:(mo + 1) * 128], rhs=yh.t[:, c, :],
                                           start=(c == 0), stop=(c == 7))
                        return ins
                    S_.op("pe", mmP, reads=[wo1.b, yh.b], writes=[pq.b])
                    S_.op("dve", lambda e, mo=mo, pq=pq: e.tensor_tensor(out=X.t[:, mo, :], in0=pq.t[:, :], in1=X.t[:, mo, :], op=ALU.add),
                          reads=[pq.b, X.b], writes=[X.b])
                S_.op("pool", lambda e: e.tensor_tensor(out=sq.t[:], in0=X.t[:], in1=X.t[:], op=ALU.mult), reads=[X.b], writes=[sq.b])
                S_.op("pe", mmN, reads=[onesb.b, sq.b], writes=[psN.b])
                S_.op("dve", lambda e: e.tensor_scalar(out=rt.t[:], in0=psN.t[:, :], scalar1=1.0 / D, scalar2=EPS,
                                                       op0=ALU.mult, op1=ALU.add), reads=[psN.b], writes=[rt.b])
                S_.op("act", lambda e: e.activation(out=rt.t[:], in_=rt.t[:], func=AF.Sqrt), reads=[rt.b], writes=[rt.b])
                S_.op("dve", lambda e: e.reciprocal(out=rstd.t[:], in_=rt.t[:]), reads=[rt.b], writes=[rstd.b])
                for mo in range(8):
                    S_.op("dve", lambda e, mo=mo: e.scalar_tensor_tensor(out=X.t[:, mo, :], in0=X.t[:, mo, :],
                                                                        scalar=gam_sb.t[:, 16 + mo:17 + mo], in1=rstd.t[:],
                                                                        op0=ALU.mult, op1=ALU.mult),
                          reads=[X.b, gam_sb.b, rstd.b], writes=[X.b])
                S_.dma(lambda e, G=G: e.dma_start(out=outv[:, :, G * 512:(G + 1) * 512], in_=X.t[:]), reads=[X.b], sem_buf=X.b)
            S_.flush()
    return nc


def make_in_maps(x, norm_g, final_g, lru_w_in, lru_conv_w, lru_conv_b, lru_wa, lru_ba, lru_wx, lru_bx,
                 lru_a_param, lru_w_out, fox_w_in, fox_b_f, fox_w_out):
    f = np.float32
    x = np.asarray(x, f)

    def col(v):
        return np.ascontiguousarray(np.asarray(v, f).reshape(-1, 128).T)

    gam = np.concatenate([col(norm_g[0]), col(norm_g[1]), col(final_g)], axis=1)
    cwv = np.asarray(lru_conv_w[0], f)
    cw = np.concatenate([col(cwv[k]) for k in range(4)], axis=1)
    vec = np.concatenate([col(lru_conv_b[0]), col(lru_ba[0]), col(lru_bx[0]), col(lru_a_param[0])], axis=1)
    ident = np.eye(128, dtype=f)
    kk = np.arange(128)[:, None]
    cc = np.arange(128)[None, :]
    tri = np.where(kk <= cc, 0.0, NEG).astype(f)
    common = {
        "gam": np.ascontiguousarray(gam), "w_in0": np.ascontiguousarray(lru_w_in[0], f), "cw": np.ascontiguousarray(cw),
        "vec": np.ascontiguousarray(vec), "wa": np.ascontiguousarray(lru_wa[0], f), "wx": np.ascontiguousarray(lru_wx[0], f),
        "w_out0": np.ascontiguousarray(lru_w_out[0], f), "w_in1": np.ascontiguousarray(fox_w_in[0], f),
        "bfv": np.ascontiguousarray(np.asarray(fox_b_f[0], f).reshape(16, 1)),
        "w_out1": np.ascontiguousarray(fox_w_out[0], f), "ident": ident, "tri": tri,
    }
    maps = []
    for b in range(2):
        xbT = np.ascontiguousarray(x[b].T)
        for j in range(4):
            P = 128 * (3 - j)
            xT = np.zeros((D, S), f)
            xT[:, P:] = xbT[:, :S - P]
            tokmask = np.ones((128, 512), f)
            tokmask[:, :P] = 0.0
            lsub = np.zeros((16, 512), f)
            if P > 0:
                lsub[:, 0] = NEG
                lsub[:, P] = -NEG
            m = dict(common)
            m.update({"xT": xT, "tokmask": tokmask, "lsub": lsub})
            maps.append(m)
    return maps


_NC_CACHE = {}


def kernel(**inputs):
    maps = make_in_maps(**inputs)
    if "nc" not in _NC_CACHE:
        _NC_CACHE["nc"] = build()
    nc = _NC_CACHE["nc"]
    res = run_bass_kernel_spmd(nc, maps, core_ids=list(range(8)))
    outp = np.zeros((2, S, D), np.float32)
    for b in range(2):
        for j in range(4):
            o = np.asarray(res.results[b * 4 + j]["out"])
            for s in range(16):
                t = 128 * (4 * s + j)
                outp[b, t:t + 128, :] = o[:, s * 128:(s + 1) * 128].T
    return outp
```

```python
import contextlib
import numpy as np
import concourse.bass as bass
import concourse.mybir as mybir
from concourse.bass_utils import run_bass_kernel_spmd

F32 = mybir.dt.float32
BF16 = mybir.dt.bfloat16
AF = mybir.ActivationFunctionType
ALU = mybir.AluOpType

D = 1024
S = 8192
W = 1536
EPS = 1e-6
NEG = -30000.0


class Buf:
    __slots__ = ("name", "last_w", "readers", "dma_readers", "sem", "cnt")

    def __init__(self, name):
        self.name = name
        self.last_w = None
        self.readers = {}
        self.dma_readers = []
        self.sem = None
        self.cnt = 0


class Op:
    __slots__ = ("eng", "fn", "deps", "sem", "val", "is_dma", "needs_inc")

    def __init__(self, eng, fn, is_dma=False):
        self.eng = eng
        self.fn = fn
        self.deps = []
        self.sem = None
        self.val = 0
        self.is_dma = is_dma
        self.needs_inc = False


ENGS = ("pe", "act", "dve", "pool", "sp")
SEM_ROT = 30000


class Sched:
    def __init__(self, nc, stack):
        self.nc = nc
        self.stack = stack
        self.ops = []
        self.eng_sem = {}
        self.eng_cnt = {e: 0 for e in ENGS}
        self.sem_final = {}
        self.waited = {e: {} for e in ENGS}
        self.nsem = 0
        for e in ENGS:
            self._new_eng_sem(e)

    def _alloc_sem(self, name):
        self.nsem += 1
        return self.stack.enter_context(self.nc.semaphore(f"{name}_{self.nsem}"))

    def _new_eng_sem(self, e):
        self.eng_sem[e] = self._alloc_sem("e" + e)
        self.eng_cnt[e] = 0

    def _track(self, op, reads, writes):
        deps = []
        seen = set()

        def add(d):
            if d is not None and id(d) not in seen and d is not op:
                seen.add(id(d))
                deps.append(d)

        for b in list(reads) + list(writes):
            add(b.last_w)
        for b in writes:
            for r in b.readers.values():
                add(r)
            for r in b.dma_readers:
                add(r)
        wset = set(id(b) for b in writes)
        for b in writes:
            b.last_w = op
            b.readers = {}
            b.dma_readers = []
        for b in reads:
            if id(b) in wset:
                continue
            if op.is_dma:
                b.dma_readers.append(op)
            else:
                b.readers[op.eng] = op
        for d in deps:
            if d.is_dma:
                continue
            if d.eng == op.eng and d.eng == "pe" and not op.is_dma:
                continue
            d.needs_inc = True
        op.deps = deps

    def op(self, eng, fn, reads=(), writes=()):
        o = Op(eng, fn)
        self._track(o, reads, writes)
        self.ops.append(o)
        return o

    def dma(self, fn, reads=(), writes=(), sem_buf=None, queue="sp"):
        o = Op(queue, fn, is_dma=True)
        self._track(o, reads, writes)
        if sem_buf.sem is None:
            sem_buf.sem = self._alloc_sem("d")
            sem_buf.cnt = 0
        sem_buf.cnt += 1
        o.sem = sem_buf.sem
        o.val = 16 * sem_buf.cnt
        self.sem_final[id(o.sem)] = (o.sem, o.val)
        self.ops.append(o)
        return o

    def flush(self):
        nc = self.nc
        ops = self.ops
        self.ops = []
        for e in ENGS:
            for o in reversed(ops):
                if o.eng == e and not o.is_dma:
                    o.needs_inc = True
                    break
        for o in ops:
            if o.is_dma:
                continue
            if o.needs_inc:
                if self.eng_cnt[o.eng] >= SEM_ROT:
                    self._new_eng_sem(o.eng)
                self.eng_cnt[o.eng] += 1
                o.sem = self.eng_sem[o.eng]
                o.val = self.eng_cnt[o.eng]
                self.sem_final[id(o.sem)] = (o.sem, o.val)
        finals = list(self.sem_final.values())
        per_eng = {e: [o for o in ops if o.eng == e] for e in ENGS}

        def emit(e, eng):
            waited = self.waited[e]
            for o in per_eng[e]:
                for d in o.deps:
                    if d.sem is None:
                        continue
                    if d.eng == e and e == "pe" and not d.is_dma and not o.is_dma:
                        continue
                    k = id(d.sem)
                    if waited.get(k, 0) < d.val:
                        eng.wait_ge(d.sem, d.val)
                        waited[k] = d.val
                ins = o.fn(eng)
                if o.is_dma:
                    ins.then_inc(o.sem, 16)
                elif o.needs_inc:
                    ins.then_inc(o.sem, 1)
            for (s, v) in finals:
                k = id(s)
                if waited.get(k, 0) < v:
                    eng.wait_ge(s, v)
                    waited[k] = v

        with nc.Block() as block:
            @block.tensor
            def _(eng):
                emit("pe", eng)

            @block.scalar
            def _(eng):
                emit("act", eng)

            @block.vector
            def _(eng):
                emit("dve", eng)

            @block.gpsimd
            def _(eng):
                emit("pool", eng)

            @block.sync
            def _(eng):
                emit("sp", eng)


class T:
    def __init__(self, t, name, n=1):
        self.t = t
        self.bs = [Buf(f"{name}_{i}") for i in range(n)]
        self.b = self.bs[0]


def build(phases=3, dbg=False):
    nc = bass.Bass("TRN2", target_bir_lowering=False)

    def din(name, shape):
        return nc.dram_tensor(name, shape, F32, kind="ExternalInput").ap()

    xT = din("xT", [D, S])
    gam = din("gam", [128, 24])
    w_in0 = din("w_in0", [D, 3072])
    cw = din("cw", [128, 48])
    vec = din("vec", [128, 48])
    wa = din("wa", [12, 128, 128])
    wx = din("wx", [12, 128, 128])
    w_out0 = din("w_out0", [W, D])
    w_in1 = din("w_in1", [D, 4112])
    bfv = din("bfv", [16, 1])
    w_out1 = din("w_out1", [D, D])
    ident_d = din("ident", [128, 128])
    tri_d = din("tri", [128, 128])
    tokmask_d = din("tokmask", [128, 512])
    lsub_d = din("lsub", [16, 512])
    out = nc.dram_tensor("out", [D, 2048], F32, kind="ExternalOutput").ap()
    skind = "ExternalOutput" if dbg else "Internal"
    ys = nc.dram_tensor("ys", [W, S], BF16, kind=skind).ap()
    x1o = nc.dram_tensor("x1o", [D, 2048], F32, kind=skind).ap()
    ks = nc.dram_tensor("ks", [D, S], BF16, kind=skind).ap()
    vs = nc.dram_tensor("vs", [16, 128, 64, 128], BF16, kind=skind).ap()
    cpos = nc.dram_tensor("cpos", [16, 3, S], BF16, kind=skind).ap()
    cneg = nc.dram_tensor("cneg", [16, 3, S], BF16, kind=skind).ap()

    with contextlib.ExitStack() as gst:
        S_ = Sched(nc, gst)

        with contextlib.ExitStack() as st:
            def sb(name, shape, dt, n=1):
                return T(st.enter_context(nc.sbuf_tensor(name, shape, dt)), name, n)

            TT = 256
            NT1 = S // TT
            NCH = 12
            NI = NT1 * NCH
            w0 = sb("w0", [128, 8, 3072], BF16)
            wab = sb("wab", [128, 12, 128], BF16)
            wxb = sb("wxb", [128, 12, 128], BF16)
            dg = sb("dg", [128, 48, 128], BF16)
            gam_sb = sb("gam_sb", [128, 24], F32)
            vec_sb = sb("vec_sb", [128, 48], F32)
            onesb = sb("onesb", [128, 128], BF16)
            hc = sb("hc", [128, 12], F32)
            hb = sb("hb", [128, 24], F32)
            ctmp = sb("ctmp", [128, 12], F32)
            tokm = sb("tokm", [128, 512], BF16)
            X = [sb(f"X{i}", [128, 8, TT], F32) for i in range(2)]
            sq = sb("sq", [128, 8, TT], BF16)
            xbf = [sb(f"xbf{i}", [128, 8, TT], BF16, 8) for i in range(2)]
            rt = sb("rt", [128, TT], F32)
            rstd = [sb(f"rstd{i}", [128, TT], F32) for i in range(2)]
            rstdh = [sb(f"rstdh{i}", [128, TT], F32) for i in range(2)]
            xb = sb("xb", [128, 12, TT + 3], BF16, 12)
            NXC, NXCB, NTHR, NTHI, NGH, NTHG = 5, 3, 2, 2, 3, 2
            xcr = [sb(f"xcr{i}", [128, TT], F32) for i in range(NXC)]
            xcbr = [sb(f"xcbr{i}", [128, TT], BF16) for i in range(NXCB)]
            thr = [sb(f"thr{i}", [128, TT], F32) for i in range(NTHR)]
            thi = [sb(f"thi{i}", [128, TT], F32) for i in range(NTHI)]
            ghr = [sb(f"ghr{i}", [128, TT], F32) for i in range(NGH)]
            thg = [sb(f"thg{i}", [128, TT], F32) for i in range(NTHG)]
            aa = [sb(f"aa{i}", [128, 12, TT], F32, 12) for i in range(2)]
            u1 = [sb(f"u1{i}", [128, 12, TT], F32, 12) for i in range(2)]
            sg = [sb(f"sg{i}", [128, 12, TT], BF16, 12) for i in range(2)]
            tmp = sb("tmp", [128, 12, TT], F32, 12)
            hprev = sb("hprev", [128, 12], F32, 12)
            yb = [sb(f"yb{i}", [128, 12, TT], BF16, 12) for i in range(2)]
            banks = [st.enter_context(nc.psum_tensor(f"pbA{i}", [128, 512], F32)) for i in range(8)]
            bank_bufs = [Buf(f"pbA{i}") for i in range(8)]

            class PSl:
                def __init__(self, bank, half):
                    self.bank = banks[bank]
                    self.c0 = half * TT
                    self.b = bank_bufs[bank]

                def ap(self):
                    return self.bank[:, self.c0:self.c0 + TT]

            NUG = 3
            psUs = [PSl(i, 0) for i in range(NUG)]
            psGs = [PSl(i, 1) for i in range(NUG)]
            psCs = [PSl(3, 0), PSl(4, 0)]
            psRs = [PSl(5, 0), PSl(6, 0)]
            psIs = [PSl(5, 1), PSl(6, 1)]
            psN = PSl(7, 0)

            with contextlib.ExitStack() as st0:
                identf = T(st0.enter_context(nc.sbuf_tensor("identf", [128, 128], F32)), "identf")
                cw_sb = T(st0.enter_context(nc.sbuf_tensor("cw_sb", [128, 48], F32)), "cw_sb")
                for (dst, src) in ((gam_sb, gam), (cw_sb, cw), (vec_sb, vec), (identf, ident_d)):
                    S_.dma(lambda e, d=dst, s=src: e.dma_start(out=d.t[:], in_=s[:, :]), writes=[dst.b], sem_buf=dst.b)
                for i in range(48):
                    S_.op("pool", lambda e, i=i: e.tensor_scalar(out=dg.t[:, i, :], in0=identf.t[:], scalar1=cw_sb.t[:, i:i + 1],
                                                                 scalar2=None, op0=ALU.mult),
                          reads=[identf.b, cw_sb.b], writes=[dg.b])
                S_.flush()
            S_.dma(lambda e: e.dma_start(out=tokm.t[:], in_=tokmask_d[:, :]), writes=[tokm.b], sem_buf=tokm.b, queue="pool")
            w0v = w_in0.rearrange("(k p) n -> p k n", p=128)
            stg = [aa[0], aa[1], u1[0], u1[1]]
            for k in range(8):
                sgt = stg[k % 4]
                S_.dma(lambda e, k=k, sgt=sgt: e.dma_start(out=sgt.t[:].rearrange("p c t -> p (c t)"), in_=w0v[:, k, :]),
                       writes=sgt.bs, sem_buf=sgt.bs[0])
                S_.op("dve" if k % 2 == 0 else "act",
                      (lambda e, k=k, sgt=sgt: e.tensor_scalar(out=w0.t[:, k, :], in0=sgt.t[:].rearrange("p c t -> p (c t)"),
                                                               scalar1=gam_sb.t[:, k:k + 1], scalar2=None, op0=ALU.mult)) if k % 2 == 0 else
                      (lambda e, k=k, sgt=sgt: e.activation(out=w0.t[:, k, :], in_=sgt.t[:].rearrange("p c t -> p (c t)"), func=AF.Copy,
                                                            scale=gam_sb.t[:, k:k + 1])),
                      reads=sgt.bs + [gam_sb.b], writes=[w0.b])
            S_.dma(lambda e: e.dma_start(out=wab.t[:], in_=wa.rearrange("n c d -> c n d")), writes=[wab.b], sem_buf=wab.b, queue="pool")
            S_.dma(lambda e: e.dma_start(out=wxb.t[:], in_=wx.rearrange("n c d -> c n d")), writes=[wxb.b], sem_buf=wxb.b, queue="pool")
            S_.op("pool", lambda e: e.memset(onesb.t[:], 1.0), writes=[onesb.b])
            S_.op("pool", lambda e: e.memset(hprev.t[:], 0.0), writes=hprev.bs)
            S_.op("pool", lambda e: e.memset(xb.t[:], 0.0), writes=xb.bs)
            S_.op("act", lambda e: e.activation(out=ctmp.t[:], in_=vec_sb.t[:, 36:48], func=AF.Exp, scale=-1.0),
                  reads=[vec_sb.b], writes=[ctmp.b])
            S_.op("act", lambda e: e.activation(out=ctmp.t[:], in_=ctmp.t[:], func=AF.Ln, bias=1.0),
                  reads=[ctmp.b], writes=[ctmp.b])
            S_.op("dve", lambda e: e.tensor_scalar(out=hc.t[:], in0=ctmp.t[:], scalar1=-4.0, scalar2=None, op0=ALU.mult),
                  reads=[ctmp.b], writes=[hc.b])
            S_.op("dve", lambda e: e.tensor_scalar(out=hb.t[:], in0=vec_sb.t[:, 12:36], scalar1=0.5, scalar2=None, op0=ALU.mult),
                  reads=[vec_sb.b], writes=[hb.b])

            xTv = xT.rearrange("(k p) t -> p k t", p=128)
            ysv = ys.rearrange("(c p) t -> p c t", p=128)

            def load_x(i):
                Xc = X[i % 2]
                S_.dma(lambda e, t0=i * TT, Xc=Xc: e.dma_start(out=Xc.t[:], in_=xTv[:, :, t0:t0 + TT]), writes=[Xc.b], sem_buf=Xc.b)

            def casts(i):
                xbc = xbf[i % 2]
                S_.dma(lambda e, t0=i * TT, xbc=xbc: e.dma_start(out=xbc.t[:], in_=xTv[:, :, t0:t0 + TT]), writes=xbc.bs,
                       sem_buf=xbc.bs[0], queue="pool")

            def norm_a(i):
                Xc = X[i % 2]
                S_.op("pool", lambda e, Xc=Xc: e.tensor_tensor(out=sq.t[:], in0=Xc.t[:], in1=Xc.t[:], op=ALU.mult),
                      reads=[Xc.b], writes=[sq.b])

                def mmN(e):
                    for k in range(8):
                        ins = e.matmul(psN.ap(), lhsT=onesb.t[:], rhs=sq.t[:, k, :], start=(k == 0), stop=(k == 7))
                    return ins
                S_.op("pe", mmN, reads=[onesb.b, sq.b], writes=[psN.b])
                S_.op("dve", lambda e: e.tensor_scalar(out=rt.t[:], in0=psN.ap(), scalar1=1.0 / D, scalar2=EPS,
                                                       op0=ALU.mult, op1=ALU.add), reads=[psN.b], writes=[rt.b])

            def norm_sqrt():
                S_.op("act", lambda e: e.activation(out=rt.t[:], in_=rt.t[:], func=AF.Sqrt), reads=[rt.b], writes=[rt.b])

            def norm_b(i):
                r_, rh_ = rstd[i % 2], rstdh[i % 2]
                S_.op("dve", lambda e, r_=r_: e.reciprocal(out=r_.t[:], in_=rt.t[:]), reads=[rt.b], writes=[r_.b])
                S_.op("dve", lambda e, r_=r_, rh_=rh_: e.tensor_scalar(out=rh_.t[:], in0=r_.t[:], scalar1=0.5, scalar2=None, op0=ALU.mult),
                      reads=[r_.b], writes=[rh_.b])

            def op_UG(it, m):
                xbc = xbf[it % 2]
                n = it * NCH + m
                pu, pg = psUs[n % NUG], psGs[n % NUG]

                def mmU(e):
                    for k in range(8):
                        ins = e.matmul(pu.ap(), lhsT=w0.t[:, k, m * 128:(m + 1) * 128], rhs=xbc.t[:, k, :], start=(k == 0), stop=(k == 7))
                    return ins
                S_.op("pe", mmU, reads=[w0.b] + xbc.bs, writes=[pu.b])

                def mmG(e):
                    for k in range(8):
                        ins = e.matmul(pg.ap(), lhsT=w0.t[:, k, W + m * 128:W + (m + 1) * 128], rhs=xbc.t[:, k, :], start=(k == 0), stop=(k == 7))
                    return ins
                S_.op("pe", mmG, reads=[w0.b] + xbc.bs, writes=[pg.b])

            def op_EV(it, m):
                n = it * NCH + m
                pu, pg = psUs[n % NUG], psGs[n % NUG]
                r_, rh_ = rstd[it % 2], rstdh[it % 2]
                g_ = ghr[n % NGH]
                S_.op("dve", lambda e: e.tensor_tensor(out=xb.t[:, m, 3:3 + TT], in0=pu.ap(), in1=r_.t[:], op=ALU.mult),
                      reads=[pu.b, r_.b], writes=[xb.bs[m]])
                S_.op("dve", lambda e: e.tensor_tensor(out=g_.t[:], in0=pg.ap(), in1=rh_.t[:], op=ALU.mult),
                      reads=[pg.b, rh_.b], writes=[g_.b])

            def op_C(it, m):
                n = it * NCH + m
                pc = psCs[n % 2]

                def mmC(e):
                    for k in range(4):
                        ins = e.matmul(pc.ap(), lhsT=dg.t[:, k * 12 + m, :], rhs=xb.t[:, m, k:k + TT], start=(k == 0), stop=(k == 3))
                    return ins
                S_.op("pe", mmC, reads=[dg.b, xb.bs[m]], writes=[pc.b])

            def op_THG(it, m):
                n = it * NCH + m
                g_, tg_ = ghr[n % NGH], thg[n % NTHG]
                S_.op("act", lambda e: e.activation(out=tg_.t[:], in_=g_.t[:], func=AF.Tanh), reads=[g_.b], writes=[tg_.b])

            def op_SG(it, m):
                n = it * NCH + m
                g_, tg_, sgc = ghr[n % NGH], thg[n % NTHG], sg[it % 2]
                S_.op("dve", lambda e: e.scalar_tensor_tensor(out=sgc.t[:, m, :], in0=tg_.t[:], scalar=1.0, in1=g_.t[:],
                                                              op0=ALU.add, op1=ALU.mult),
                      reads=[g_.b, tg_.b], writes=[sgc.bs[m]])

            def op_XC(it, m):
                n = it * NCH + m
                pc, xc_ = psCs[n % 2], xcr[n % NXC]
                S_.op("act", lambda e: e.activation(out=xc_.t[:], in_=pc.ap(), func=AF.Identity, bias=vec_sb.t[:, m:m + 1]),
                      reads=[pc.b, vec_sb.b], writes=[xc_.b])

            def op_HALO(it, m):
                S_.op("pool", lambda e: e.tensor_copy(out=xb.t[:, m, 0:3], in_=xb.t[:, m, TT:TT + 3]),
                      reads=[xb.bs[m]], writes=[xb.bs[m]])

            def op_XCB(it, m):
                n = it * NCH + m
                pc, xcb_ = psCs[n % 2], xcbr[n % NXCB]
                S_.op("dve", lambda e: e.tensor_scalar(out=xcb_.t[:], in0=pc.ap(), scalar1=vec_sb.t[:, m:m + 1], scalar2=None, op0=ALU.add),
                      reads=[vec_sb.b], writes=[xcb_.b, pc.b])

            def op_RI(it, m):
                n = it * NCH + m
                pr, pi, xcb_ = psRs[n % 2], psIs[n % 2], xcbr[n % NXCB]
                S_.op("pe", lambda e: e.matmul(pr.ap(), lhsT=wab.t[:, m, :], rhs=xcb_.t[:], start=True, stop=True),
                      reads=[wab.b, xcb_.b], writes=[pr.b])
                S_.op("pe", lambda e: e.matmul(pi.ap(), lhsT=wxb.t[:, m, :], rhs=xcb_.t[:], start=True, stop=True),
                      reads=[wxb.b, xcb_.b], writes=[pi.b])

            def op_TH(it, m):
                n = it * NCH + m
                pr, pi, tr_, ti_ = psRs[n % 2], psIs[n % 2], thr[n % NTHR], thi[n % NTHI]
                S_.op("act", lambda e: e.activation(out=tr_.t[:], in_=pr.ap(), func=AF.Tanh, bias=hb.t[:, m:m + 1], scale=0.5),
                      reads=[pr.b, hb.b], writes=[tr_.b])
                S_.op("act", lambda e: e.activation(out=ti_.t[:], in_=pi.ap(), func=AF.Tanh, bias=hb.t[:, 12 + m:13 + m], scale=0.5),
                      reads=[pi.b, hb.b], writes=[ti_.b])

            def op_AA(it, m):
                n = it * NCH + m
                tr_, aac = thr[n % NTHR], aa[it % 2]
                S_.op("act", lambda e: e.activation(out=aac.t[:, m, :], in_=tr_.t[:], func=AF.Exp, scale=hc.t[:, m:m + 1], bias=hc.t[:, m:m + 1]),
                      reads=[tr_.b, hc.b], writes=[aac.bs[m]])

            def op_U1(it, m):
                n = it * NCH + m
                xc_, ti_, u1c = xcr[n % NXC], thi[n % NTHI], u1[it % 2]
                S_.op("dve", lambda e: e.scalar_tensor_tensor(out=u1c.t[:, m, :], in0=ti_.t[:], scalar=1.0, in1=xc_.t[:],
                                                              op0=ALU.add, op1=ALU.mult),
                      reads=[xc_.b, ti_.b], writes=[u1c.bs[m]])

            def op_MID(it, m):
                u1p, aap, sgp, ybc = u1[it % 2], aa[it % 2], sg[it % 2], yb[it % 2]
                t0p = it * TT
                S_.op("pool", lambda e: e.tensor_tensor(out=u1p.t[:, m, :], in0=u1p.t[:, m, :], in1=tmp.t[:, m, :], op=ALU.mult),
                      reads=[u1p.bs[m], tmp.bs[m]], writes=[u1p.bs[m]])
                if t0p < 512:
                    S_.op("pool", lambda e: e.tensor_tensor(out=u1p.t[:, m, :], in0=u1p.t[:, m, :], in1=tokm.t[:, t0p:t0p + TT], op=ALU.mult),
                          reads=[u1p.bs[m], tokm.b], writes=[u1p.bs[m]])
                S_.op("dve", lambda e: e.tensor_tensor_scan(out=tmp.t[:, m, :], data0=aap.t[:, m, :], data1=u1p.t[:, m, :],
                                                            initial=hprev.t[:, m:m + 1], op0=ALU.mult, op1=ALU.add),
                      reads=[aap.bs[m], u1p.bs[m], hprev.bs[m], tmp.bs[m]], writes=[tmp.bs[m]])
                S_.op("pool", lambda e: e.tensor_copy(out=hprev.t[:, m:m + 1], in_=tmp.t[:, m, TT - 1:TT]),
                      reads=[tmp.bs[m]], writes=[hprev.bs[m]])
                S_.op("pool", lambda e: e.tensor_tensor(out=ybc.t[:, m, :], in0=tmp.t[:, m, :], in1=sgp.t[:, m, :], op=ALU.mult),
                      reads=[tmp.bs[m], sgp.bs[m]], writes=[ybc.bs[m]])
                if m == NCH - 1:
                    S_.dma(lambda e: e.dma_start(out=ysv[:, :, t0p:t0p + TT], in_=ybc.t[:]), reads=ybc.bs, sem_buf=ybc.bs[0])

            def batch(it):
                aap = aa[it % 2]
                S_.op("act", lambda e: e.activation(out=tmp.t[:], in_=aap.t[:], func=AF.Square), reads=aap.bs, writes=tmp.bs)
                S_.op("act", lambda e: e.activation(out=tmp.t[:], in_=tmp.t[:], func=AF.Sqrt, scale=-0.25, bias=0.25),
                      reads=tmp.bs, writes=tmp.bs)

            LAGS = [(0, op_UG), (1, op_EV), (2, op_C), (2, op_THG), (3, op_SG), (3, op_XC), (3, op_HALO), (3, op_XCB),
                    (4, op_RI), (5, op_TH), (6, op_AA), (6, op_U1), (19, op_MID)]
            load_x(0)
            load_x(1)
            casts(0)
            norm_a(0)
            norm_sqrt()
            norm_b(0)
            load_x(2)
            for g in range(NI + 20):
                for (L, fn) in LAGS:
                    n = g - L
                    if 0 <= n < NI:
                        fn(n // NCH, n % NCH)
                t, r = divmod(g, NCH)
                if r == 4 and t + 1 < NT1:
                    norm_a(t + 1)
                if r == 6:
                    if t + 1 < NT1:
                        norm_sqrt()
                    if 0 <= t - 1 < NT1:
                        batch(t - 1)
                    if t + 1 < NT1:
                        norm_b(t + 1)
                if r == 9 and t + 1 < NT1:
                    casts(t + 1)
                    if t + 3 < NT1:
                        load_x(t + 3)
            S_.flush()

        if phases < 2:
            return nc
        with contextlib.ExitStack() as st:
            def sb(name, shape, dt, n=1):
                return T(st.enter_context(nc.sbuf_tensor(name, shape, dt)), name, n)

            def ps(name):
                return T(st.enter_context(nc.psum_tensor(name, [128, 512], F32)), name)

            TT = 512
            NT2 = S // TT
            wo0 = sb("wo0", [128, 12, 1024], BF16)
            wk = sb("wk", [128, 8, 1024], BF16)
            wv = sb("wv", [128, 8, 1024], BF16)
            wf = sb("wf", [128, 8, 16], BF16)
            gam_sb = sb("gam_sb2", [128, 24], F32)
            onesb = sb("onesb2", [128, 128], BF16)
            ones16 = sb("ones16", [16, 512], F32)
            nbf = sb("nbf", [16, 1], F32)
            lsub = sb("lsub_sb", [16, 512], F32)
            X = [sb(f"X2_{i}", [128, 8, TT], F32, 8) for i in range(2)]
            Y = [sb(f"Y2_{i}", [128, 12, TT], BF16) for i in range(2)]
            sq = [sb(f"sq2_{i}", [128, 8, TT], BF16, 8) for i in range(2)]
            xbf = [sb(f"xbf2_{i}", [128, 8, TT], BF16, 8) for i in range(2)]
            rt = sb("rt2", [128, TT], F32)
            rstd = [sb(f"rstd2_{i}", [128, TT], F32) for i in range(2)]
            rtT = sb("rtT", [128, 4], F32)
            rstdT = [sb(f"rstdT{i}", [128, 4], F32) for i in range(2)]
            kst = sb("kst", [128, 8, TT], BF16)
            vst = sb("vst", [128, 8, 2, 4, 128], BF16)
            zf = sb("zf", [16, 512], F32)
            cc = sb("cc", [16, 512], F32)
            r1 = sb("r1", [16, 512], F32)
            cprev = sb("cprev", [16, 1], F32)
            Pp = sb("Pp", [16, 3, 512], BF16)
            Pn = sb("Pn", [16, 3, 512], BF16)
            psO = [ps("psO0"), ps("psO1")]
            psN = ps("psN2")
            psT = ps("psT2")
            psK = [ps("psK0"), ps("psK1")]
            psV = [ps("psV0"), ps("psV1")]

            w1v = w_in1.rearrange("(k p) n -> p k n", p=128)
            wo0v = w_out0.rearrange("(k p) n -> p k n", p=128)
            for k in range(12):
                S_.dma(lambda e, k=k: e.dma_start(out=wo0.t[:, k, :], in_=wo0v[:, k, :]), writes=[wo0.b], sem_buf=wo0.b, queue="pool")
            for k in range(8):
                S_.dma(lambda e, k=k: e.dma_start(out=wk.t[:, k, :], in_=w1v[:, k, 1024:2048]), writes=[wk.b], sem_buf=wk.b, queue="pool")
                S_.dma(lambda e, k=k: e.dma_start(out=wv.t[:, k, :], in_=w1v[:, k, 2048:3072]), writes=[wv.b], sem_buf=wv.b, queue="pool")
            S_.dma(lambda e: e.dma_start(out=wf.t[:], in_=w1v[:, :, 4096:4112]), writes=[wf.b], sem_buf=wf.b, queue="pool")
            S_.dma(lambda e: e.dma_start(out=gam_sb.t[:], in_=gam[:, :]), writes=[gam_sb.b], sem_buf=gam_sb.b)
            S_.dma(lambda e: e.dma_start(out=nbf.t[:], in_=bfv[:, :]), writes=[nbf.b], sem_buf=nbf.b)
            S_.dma(lambda e: e.dma_start(out=lsub.t[:], in_=lsub_d[:, :]), writes=[lsub.b], sem_buf=lsub.b)
            S_.op("dve", lambda e: e.tensor_scalar(out=nbf.t[:], in0=nbf.t[:], scalar1=-1.0, scalar2=None, op0=ALU.mult),
                  reads=[nbf.b], writes=[nbf.b])
            S_.op("pool", lambda e: e.memset(onesb.t[:], 1.0), writes=[onesb.b])
            S_.op("pool", lambda e: e.memset(ones16.t[:], 1.0), writes=[ones16.b])
            S_.op("pool", lambda e: e.memset(cprev.t[:], 0.0), writes=[cprev.b])
            S_.op("pool", lambda e: e.memset(vst.t[:], 1.0), writes=[vst.b])

            xTv = xT.rearrange("(k p) t -> p k t", p=128)
            ysv = ys.rearrange("(c p) t -> p c t", p=128)
            x1ov = x1o.rearrange("(k p) t -> p k t", p=128)
            ksv = ks.rearrange("(m p) t -> p m t", p=128)
            vsv = vs.rearrange("(g e) p kb c -> p g e kb c", e=2)

            def load2(i):
                Xc, Yc = X[i % 2], Y[i % 2]
                S_.dma(lambda e, t0=i * TT, Xc=Xc: e.dma_start(out=Xc.t[:], in_=xTv[:, :, t0:t0 + TT]), writes=Xc.bs, sem_buf=Xc.bs[0])
                S_.dma(lambda e, t0=i * TT, Yc=Yc: e.dma_start(out=Yc.t[:], in_=ysv[:, :, t0:t0 + TT]), writes=[Yc.b], sem_buf=Yc.b)

            def stage_o(i):
                p_ = i % 2
                Xc, Yc, sqc, xbc = X[p_], Y[p_], sq[p_], xbf[p_]
                for mo in range(8):
                    po = psO[mo % 2]

                    def mmO(e, mo=mo, po=po, Yc=Yc):
                        for c in range(12):
                            ins = e.matmul(po.t[:, :], lhsT=wo0.t[:, c, mo * 128:(mo + 1) * 128], rhs=Yc.t[:, c, :],
                                           start=(c == 0), stop=(c == 11))
                        return ins
                    S_.op("pe", mmO, reads=[wo0.b, Yc.b], writes=[po.b])
                    S_.op("dve", lambda e, mo=mo, po=po, Xc=Xc: e.tensor_tensor(out=Xc.t[:, mo, :], in0=po.t[:, :], in1=Xc.t[:, mo, :], op=ALU.add),
                          reads=[po.b, Xc.bs[mo]], writes=[Xc.bs[mo]])
                    S_.op("pool", lambda e, mo=mo, Xc=Xc, sqc=sqc: e.tensor_tensor(out=sqc.t[:, mo, :], in0=Xc.t[:, mo, :], in1=Xc.t[:, mo, :], op=ALU.mult),
                          reads=[Xc.bs[mo]], writes=[sqc.bs[mo]])
                    S_.op("act", lambda e, mo=mo, Xc=Xc, xbc=xbc: e.activation(out=xbc.t[:, mo, :], in_=Xc.t[:, mo, :], func=AF.Copy,
                                                                              scale=gam_sb.t[:, 8 + mo:9 + mo]),
                          reads=[Xc.bs[mo], gam_sb.b], writes=[xbc.bs[mo]])
                S_.dma(lambda e, i=i, Xc=Xc: e.dma_start(out=x1ov[:, :, i * 128:(i + 1) * 128], in_=Xc.t[:, :, 384:512]),
                       reads=Xc.bs, sem_buf=Xc.bs[1])

                def mmN(e, sqc=sqc):
                    for k in range(8):
                        ins = e.matmul(psN.t[:, :], lhsT=onesb.t[:], rhs=sqc.t[:, k, :], start=(k == 0), stop=(k == 7))
                    return ins
                S_.op("pe", mmN, reads=[onesb.b] + sqc.bs, writes=[psN.b])

                def mmT(e, sqc=sqc):
                    for j in range(4):
                        for k in range(8):
                            ins = e.matmul(psT.t[:, j:j + 1], lhsT=sqc.t[:, k, j * 128:(j + 1) * 128], rhs=onesb.t[:, 0:1],
                                           start=(k == 0), stop=(k == 7))
                    return ins
                S_.op("pe", mmT, reads=[onesb.b] + sqc.bs, writes=[psT.b])
                S_.op("dve", lambda e: e.tensor_scalar(out=rt.t[:], in0=psN.t[:, :], scalar1=1.0 / D, scalar2=EPS,
                                                       op0=ALU.mult, op1=ALU.add), reads=[psN.b], writes=[rt.b])
                S_.op("dve", lambda e: e.tensor_scalar(out=rtT.t[:], in0=psT.t[:, 0:4], scalar1=1.0 / D, scalar2=EPS,
                                                       op0=ALU.mult, op1=ALU.add), reads=[psT.b], writes=[rtT.b])
                S_.op("act", lambda e: e.activation(out=rt.t[:], in_=rt.t[:], func=AF.Sqrt), reads=[rt.b], writes=[rt.b])
                S_.op("act", lambda e: e.activation(out=rtT.t[:], in_=rtT.t[:], func=AF.Sqrt), reads=[rtT.b], writes=[rtT.b])
                S_.op("dve", lambda e, r_=rstd[p_]: e.reciprocal(out=r_.t[:], in_=rt.t[:]), reads=[rt.b], writes=[rstd[p_].b])
                S_.op("dve", lambda e, r_=rstdT[p_]: e.reciprocal(out=r_.t[:], in_=rtT.t[:]), reads=[rtT.b], writes=[rstdT[p_].b])

            def stage_kv(i):
                p_ = i % 2
                t0 = i * TT
                xbc, r_, rT_ = xbf[p_], rstd[p_], rstdT[p_]
                for m in range(8):
                    pk = psK[m % 2]

                    def mmK(e, m=m, pk=pk):
                        for k in range(8):
                            ins = e.matmul(pk.t[:, :], lhsT=wk.t[:, k, m * 128:(m + 1) * 128], rhs=xbc.t[:, k, :],
                                           start=(k == 0), stop=(k == 7))
                        return ins
                    S_.op("pe", mmK, reads=[wk.b] + xbc.bs, writes=[pk.b])
                    S_.op("dve", lambda e, m=m, pk=pk: e.tensor_tensor(out=kst.t[:, m, :], in0=pk.t[:, :], in1=r_.t[:], op=ALU.mult),
                          reads=[pk.b, r_.b], writes=[kst.b])
                S_.dma(lambda e, t0=t0: e.dma_start(out=ksv[:, :, t0:t0 + TT], in_=kst.t[:]), reads=[kst.b], sem_buf=kst.b)
                for j in range(4):
                    for half in range(2):
                        pv = psV[(j * 2 + half) % 2]

                        def mmV(e, j=j, half=half, pv=pv):
                            for k in range(8):
                                ins = e.matmul(pv.t[:, :], lhsT=xbc.t[:, k, j * 128:(j + 1) * 128],
                                               rhs=wv.t[:, k, half * 512:(half + 1) * 512], start=(k == 0), stop=(k == 7))
                            return ins
                        S_.op("pe", mmV, reads=[wv.b] + xbc.bs, writes=[pv.b])
                        pvv = pv.t[:, :].rearrange("p (g e d) -> p g e d", g=4, e=2, d=64)
                        S_.op("act", lambda e, j=j, half=half, pvv=pvv: e.activation(
                            out=vst.t[:, half * 4:(half + 1) * 4, 0, j, 0:64], in_=pvv[:, :, 0, :], func=AF.Copy,
                            scale=rT_.t[:, j:j + 1]), reads=[pv.b, rT_.b], writes=[vst.b])
                        S_.op("act", lambda e, j=j, half=half, pvv=pvv: e.activation(
                            out=vst.t[:, half * 4:(half + 1) * 4, 1, j, 64:128], in_=pvv[:, :, 1, :], func=AF.Copy,
                            scale=rT_.t[:, j:j + 1]), reads=[pv.b, rT_.b], writes=[vst.b])
                S_.dma(lambda e, i=i: e.dma_start(out=vsv[:, :, :, 4 * i:4 * i + 4, :], in_=vst.t[:]), reads=[vst.b], sem_buf=vst.b)

                def mmF(e):
                    for k in range(8):
                        ins = e.matmul(psT.t[0:16, :], lhsT=wf.t[:, k, :], rhs=xbc.t[:, k, :], start=(k == 0), stop=(k == 7))
                    return ins
                S_.op("pe", mmF, reads=[wf.b] + xbc.bs, writes=[psT.b])
                S_.op("dve", lambda e: e.tensor_tensor(out=zf.t[:], in0=psT.t[0:16, :], in1=r_.t[0:16, :], op=ALU.mult),
                      reads=[psT.b, r_.b], writes=[zf.b])
                S_.op("act", lambda e: e.activation(out=zf.t[:], in_=zf.t[:], func=AF.Exp, scale=-1.0, bias=nbf.t[:, 0:1]),
                      reads=[zf.b, nbf.b], writes=[zf.b])
                S_.op("act", lambda e: e.activation(out=zf.t[:], in_=zf.t[:], func=AF.Ln, bias=1.0), reads=[zf.b], writes=[zf.b])
                if i == 0:
                    S_.op("dve", lambda e: e.tensor_tensor(out=zf.t[:], in0=zf.t[:], in1=lsub.t[:], op=ALU.add),
                          reads=[zf.b, lsub.b], writes=[zf.b])
                S_.op("dve", lambda e: e.tensor_tensor_scan(out=cc.t[:], data0=ones16.t[:], data1=zf.t[:], initial=cprev.t[:, 0:1],
                                                            op0=ALU.mult, op1=ALU.subtract),
                      reads=[ones16.b, zf.b, cprev.b], writes=[cc.b])
                S_.op("dve", lambda e: e.tensor_copy(out=cprev.t[:], in_=cc.t[:, TT - 1:TT]), reads=[cc.b], writes=[cprev.b])
                S_.op("dve", lambda e: e.tensor_copy(out=Pp.t[:, 0, :], in_=cc.t[:]), reads=[cc.b], writes=[Pp.b])
                S_.op("dve", lambda e: e.tensor_tensor(out=r1.t[:], in0=cc.t[:], in1=Pp.t[:, 0, :], op=ALU.subtract),
                      reads=[cc.b, Pp.b], writes=[r1.b])
                S_.op("dve", lambda e: e.tensor_copy(out=Pp.t[:, 1, :], in_=r1.t[:]), reads=[r1.b], writes=[Pp.b])
                S_.op("dve", lambda e: e.tensor_tensor(out=r1.t[:], in0=r1.t[:], in1=Pp.t[:, 1, :], op=ALU.subtract),
                      reads=[r1.b, Pp.b], writes=[r1.b])
                S_.op("dve", lambda e: e.tensor_copy(out=Pp.t[:, 2, :], in_=r1.t[:]), reads=[r1.b], writes=[Pp.b])
                S_.op("dve", lambda e: e.tensor_scalar(out=Pn.t[:], in0=Pp.t[:], scalar1=-1.0, scalar2=None, op0=ALU.mult),
                      reads=[Pp.b], writes=[Pn.b])
                S_.dma(lambda e, t0=t0: e.dma_start(out=cpos[:, :, t0:t0 + TT], in_=Pp.t[:]), reads=[Pp.b], sem_buf=Pp.b)
                S_.dma(lambda e, t0=t0: e.dma_start(out=cneg[:, :, t0:t0 + TT], in_=Pn.t[:]), reads=[Pn.b], sem_buf=Pn.b)

            load2(0)
            for it in range(NT2 + 1):
                if it + 1 < NT2:
                    load2(it + 1)
                if it < NT2:
                    stage_o(it)
                if it >= 1:
                    stage_kv(it - 1)
            S_.flush()

        if phases < 3:
            return nc
        with contextlib.ExitStack() as st:
            def sb(name, shape, dt):
                return T(st.enter_context(nc.sbuf_tensor(name, shape, dt)), name)

            def ps(name):
                return T(st.enter_context(nc.psum_tensor(name, [128, 512], F32)), name)

            TT = 512
            wq = sb("wq", [128, 8, 1024], BF16)
            wg = sb("wg", [128, 8, 1024], BF16)
            wo1 = sb("wo1", [128, 8, 1024], BF16)
            gam_sb = sb("gam_sb3", [128, 24], F32)
            onesb = sb("onesb3", [128, 128], BF16)
            identb = sb("identb", [128, 128], BF16)
            trib = sb("trib", [128, 128], BF16)
            X = sb("X3", [128, 8, TT], F32)
            sq = sb("sq3", [128, 8, TT], BF16)
            xbf = sb("xbf3", [128, 8, TT], BF16)
            rt = sb("rt3", [128, TT], F32)
            rstd = sb("rstd3", [128, TT], F32)
            rq = sb("rq3", [128, TT], F32)
            tq = [sb(f"tq{i}", [128, TT], F32) for i in range(2)]
            qa = sb("qa", [70, 16, TT], BF16)
            sg = sb("sg3", [128, 8, TT], BF16)
            yh = sb("yh", [128, 8, TT], BF16)
            ka = [sb(f"ka{i}", [70, S], BF16) for i in range(2)]
            va = [sb(f"va{i}", [128, 64, 128], BF16) for i in range(2)]
            pt = [sb(f"pt{i}", [128, TT], BF16) for i in range(5)]
            rl = sb("rl", [128, TT], F32)
            on = sb("on", [128, TT], F32)
            psN = ps("psN3")
            psS = [ps(f"psS{i}") for i in range(5)]
            psQ = [psS[0], psS[1]]
            psOo = [ps("psOb0"), ps("psOb1")]

            w1v = w_in1.rearrange("(k p) n -> p k n", p=128)
            wo1v = w_out1.rearrange("(k p) n -> p k n", p=128)
            for k in range(8):
                S_.dma(lambda e, k=k: e.dma_start(out=wq.t[:, k, :], in_=w1v[:, k, 0:1024]), writes=[wq.b], sem_buf=wq.b, queue="pool")
                S_.dma(lambda e, k=k: e.dma_start(out=wg.t[:, k, :], in_=w1v[:, k, 3072:4096]), writes=[wg.b], sem_buf=wg.b, queue="pool")
                S_.dma(lambda e, k=k: e.dma_start(out=wo1.t[:, k, :], in_=wo1v[:, k, :]), writes=[wo1.b], sem_buf=wo1.b, queue="pool")
            S_.dma(lambda e: e.dma_start(out=identb.t[:], in_=ident_d[:, :]), writes=[identb.b], sem_buf=identb.b, queue="pool")
            S_.dma(lambda e: e.dma_start(out=trib.t[:], in_=tri_d[:, :]), writes=[trib.b], sem_buf=trib.b, queue="pool")
            S_.dma(lambda e: e.dma_start(out=gam_sb.t[:], in_=gam[:, :]), writes=[gam_sb.b], sem_buf=gam_sb.b)
            S_.op("pool", lambda e: e.memset(onesb.t[:], 1.0), writes=[onesb.b])
            S_.op("pool", lambda e: e.memset(qa.t[64:70, :, :], 1.0), writes=[qa.b])
            for i in range(2):
                S_.op("pool", lambda e, i=i: e.memset(ka[i].t[64:70, :], 1.0), writes=[ka[i].b])

            x1ov = x1o.rearrange("(k p) t -> p k t", p=128)
            cpg = cpos.rearrange("h r (t c) -> r h t c", c=512)
            outv = out.rearrange("(k p) t -> p k t", p=128)
            cnt_s = 0
            cnt_p = 0
            xsem = [Buf(f"xsem{i}") for i in range(4)]
            qsem = [Buf(f"qsem{i}") for i in range(4)]
            for G in range(4):
                L = 2048 * (G + 1)
                nkb = 16 * (G + 1)
                S_.dma(lambda e, G=G: e.dma_start(out=X.t[:], in_=x1ov[:, :, G * 512:(G + 1) * 512]), writes=[X.b], sem_buf=X.b)
                S_.op("pool", lambda e: e.tensor_tensor(out=sq.t[:], in0=X.t[:], in1=X.t[:], op=ALU.mult), reads=[X.b], writes=[sq.b])
                for k in range(8):
                    S_.op("act", lambda e, k=k: e.activation(out=xbf.t[:, k, :], in_=X.t[:, k, :], func=AF.Copy,
                                                             scale=gam_sb.t[:, 8 + k:9 + k]),
                          reads=[X.b, gam_sb.b], writes=[xbf.b])

                def mmN(e):
                    for k in range(8):
                        ins = e.matmul(psN.t[:, :], lhsT=onesb.t[:], rhs=sq.t[:, k, :], start=(k == 0), stop=(k == 7))
                    return ins
                S_.op("pe", mmN, reads=[onesb.b, sq.b], writes=[psN.b])
                S_.op("dve", lambda e: e.tensor_scalar(out=rt.t[:], in0=psN.t[:, :], scalar1=1.0 / D, scalar2=EPS,
                                                       op0=ALU.mult, op1=ALU.add), reads=[psN.b], writes=[rt.b])
                S_.op("act", lambda e: e.activation(out=rt.t[:], in_=rt.t[:], func=AF.Sqrt), reads=[rt.b], writes=[rt.b])
                S_.op("dve", lambda e: e.reciprocal(out=rstd.t[:], in_=rt.t[:]), reads=[rt.b], writes=[rstd.b])
                S_.op("dve", lambda e: e.tensor_scalar(out=rq.t[:], in0=rstd.t[:], scalar1=0.125, scalar2=None, op0=ALU.mult),
                      reads=[rstd.b], writes=[rq.b])
                for m in range(8):
                    pq = psQ[m % 2]
                    tqm = tq[m % 2]

                    def mmQ(e, m=m, pq=pq):
                        for k in range(8):
                            ins = e.matmul(pq.t[:, :], lhsT=wq.t[:, k, m * 128:(m + 1) * 128], rhs=xbf.t[:, k, :],
                                           start=(k == 0), stop=(k == 7))
                        return ins
                    S_.op("pe", mmQ, reads=[wq.b, xbf.b], writes=[pq.b])
                    S_.op("dve", lambda e, pq=pq, tqm=tqm: e.tensor_tensor(out=tqm.t[:], in0=pq.t[:, :], in1=rq.t[:], op=ALU.mult),
                          reads=[pq.b, rq.b], writes=[tqm.b])
                    S_.op("act", lambda e, m=m, tqm=tqm: e.activation(out=qa.t[0:64, 2 * m, :], in_=tqm.t[0:64, :], func=AF.Copy),
                          reads=[tqm.b], writes=[qa.b])
                    S_.op("act", lambda e, m=m, tqm=tqm: e.activation(out=qa.t[0:64, 2 * m + 1, :], in_=tqm.t[64:128, :], func=AF.Copy),
                          reads=[tqm.b], writes=[qa.b])
                for a_ in range(4):
                    S_.dma(lambda e, G=G, a_=a_: e.dma_start(out=qa.t[67:70, :, a_ * 128:(a_ + 1) * 128],
                                                             in_=cpg[:, :, 4 * G + a_, 384:512]), writes=[qa.b], sem_buf=qsem[a_])
                for m in range(8):
                    pq = psQ[m % 2]
                    tqm = tq[m % 2]

                    def mmG(e, m=m, pq=pq):
                        for k in range(8):
                            ins = e.matmul(pq.t[:, :], lhsT=wg.t[:, k, m * 128:(m + 1) * 128], rhs=xbf.t[:, k, :],
                                           start=(k == 0), stop=(k == 7))
                        return ins
                    S_.op("pe", mmG, reads=[wg.b, xbf.b], writes=[pq.b])
                    S_.op("dve", lambda e, pq=pq, tqm=tqm: e.tensor_tensor(out=tqm.t[:], in0=pq.t[:, :], in1=rstd.t[:], op=ALU.mult),
                          reads=[pq.b, rstd.b], writes=[tqm.b])
                    S_.op("act", lambda e, m=m, tqm=tqm: e.activation(out=sg.t[:, m, :], in_=tqm.t[:], func=AF.Silu),
                          reads=[tqm.b], writes=[sg.b])
                for h in range(16):
                    kab = ka[h % 2]
                    vab = va[h % 2]
                    S_.dma(lambda e, h=h, kab=kab, L=L: e.dma_start(out=kab.t[0:64, 0:L], in_=ks[h * 64:(h + 1) * 64, 0:L]),
                           writes=[kab.b], sem_buf=kab.b)
                    S_.dma(lambda e, h=h, kab=kab, L=L: e.dma_start(out=kab.t[64:67, 0:L], in_=cneg[h, :, 0:L]),
                           writes=[kab.b], sem_buf=kab.b)
                    S_.dma(lambda e, h=h, vab=vab, nkb=nkb: e.dma_start(out=vab.t[:, 0:nkb, :], in_=vs[h, :, 0:nkb, :]),
                           writes=[vab.b], sem_buf=vab.b)
                    blocks = [(kb, 0, False) for kb in range(16 * G)]
                    for a_ in range(4):
                        for i in range(4):
                            blocks.append((16 * G + 4 * a_ + i, 128 * a_, i == 3))
                    po = psOo[h % 2]
                    nb = len(blocks)
                    LOOK = 2
                    pend = []

                    def emit_pv(item, po=po, vab=vab, nb=nb):
                        (bi, kb, c0, ptt) = item
                        S_.op("pe", lambda e, kb=kb, c0=c0, ptt=ptt, vab=vab, po=po, bi=bi, nb=nb: e.matmul(
                            po.t[:, c0:512], lhsT=vab.t[:, kb, :], rhs=ptt.t[:, c0:512], start=(bi == 0), stop=(bi == nb - 1)),
                            reads=[vab.b, ptt.b], writes=[po.b])

                    for bi, (kb, c0, diag) in enumerate(blocks):
                        pss = psS[cnt_s % len(psS)]
                        cnt_s += 1
                        ptt = pt[cnt_p % len(pt)]
                        cnt_p += 1

                        def mmS(e, kb=kb, c0=c0, diag=diag, pss=pss, kab=kab, h=h):
                            ins = e.matmul(pss.t[:, c0:512], lhsT=kab.t[0:70, kb * 128:(kb + 1) * 128], rhs=qa.t[0:70, h, c0:512],
                                           start=True, stop=not diag)
                            if diag:
                                ins = e.matmul(pss.t[:, c0:c0 + 128], lhsT=identb.t[:], rhs=trib.t[:], start=False, stop=True)
                            return ins
                        S_.op("pe", mmS, reads=[kab.b, qa.b, identb.b, trib.b], writes=[pss.b])
                        S_.op("act", lambda e, c0=c0, pss=pss, ptt=ptt: e.activation(out=ptt.t[:, c0:512], in_=pss.t[:, c0:512], func=AF.Exp),
                              reads=[pss.b], writes=[ptt.b])
                        pend.append((bi, kb, c0, ptt))
                        if len(pend) > LOOK:
                            emit_pv(pend.pop(0))
                    while pend:
                        emit_pv(pend.pop(0))
                    if h % 2 == 0:
                        lo, hi, lo2, hi2 = 0, 64, 64, 128
                    else:
                        lo, hi, lo2, hi2 = 64, 128, 0, 64
                    S_.op("dve", lambda e, po=po, lo2=lo2, hi2=hi2: e.reciprocal(out=rl.t[lo2:hi2, :], in_=po.t[lo2:hi2, :]),
                          reads=[po.b], writes=[rl.b])
                    S_.op("dve", lambda e, po=po, lo=lo, hi=hi, lo2=lo2, hi2=hi2: e.tensor_tensor(
                        out=on.t[lo:hi, :], in0=po.t[lo:hi, :], in1=rl.t[lo2:hi2, :], op=ALU.mult),
                        reads=[po.b, rl.b], writes=[on.b])
                    S_.op("pool", lambda e, h=h, lo=lo, hi=hi: e.tensor_tensor(
                        out=yh.t[lo:hi, h // 2, :], in0=on.t[lo:hi, :], in1=sg.t[lo:hi, h // 2, :], op=ALU.mult),
                        reads=[on.b, sg.b], writes=[yh.b])
                for mo in range(8):
                    pq = psQ[mo % 2]

                    def mmP(e, mo=mo, pq=pq):
                        for c in range(8):
                            ins = e.matmul(pq.t[:, :], lhsT=wo1.t[:, c, mo * 128:(mo + 1) * 128], rhs=yh.t[:, c, :],
                                           start=(c == 0), stop=(c == 7))
                        return ins
                    S_.op("pe", mmP, reads=[wo1.b, yh.b], writes=[pq.b])
                    S_.op("dve", lambda e, mo=mo, pq=pq: e.tensor_tensor(out=X.t[:, mo, :], in0=pq.t[:, :], in1=X.t[:, mo, :], op=ALU.add),
                          reads=[pq.b, X.b], writes=[X.b])
                S_.op("pool", lambda e: e.tensor_tensor(out=sq.t[:], in0=X.t[:], in1=X.t[:], op=ALU.mult), reads=[X.b], writes=[sq.b])
                S_.op("pe", mmN, reads=[onesb.b, sq.b], writes=[psN.b])
                S_.op("dve", lambda e: e.tensor_scalar(out=rt.t[:], in0=psN.t[:, :], scalar1=1.0 / D, scalar2=EPS,
                                                       op0=ALU.mult, op1=ALU.add), reads=[psN.b], writes=[rt.b])
                S_.op("act", lambda e: e.activation(out=rt.t[:], in_=rt.t[:], func=AF.Sqrt), reads=[rt.b], writes=[rt.b])
                S_.op("dve", lambda e: e.reciprocal(out=rstd.t[:], in_=rt.t[:]), reads=[rt.b], writes=[rstd.b])
                for mo in range(8):
                    S_.op("dve", lambda e, mo=mo: e.scalar_tensor_tensor(out=X.t[:, mo, :], in0=X.t[:, mo, :],
                                                                        scalar=gam_sb.t[:, 16 + mo:17 + mo], in1=rstd.t[:],
                                                                        op0=ALU.mult, op1=ALU.mult),
                          reads=[X.b, gam_sb.b, rstd.b], writes=[X.b])
                S_.dma(lambda e, G=G: e.dma_start(out=outv[:, :, G * 512:(G + 1) * 512], in_=X.t[:]), reads=[X.b], sem_buf=X.b)
            S_.flush()
    return nc


def make_in_maps(x, norm_g, final_g, lru_w_in, lru_conv_w, lru_conv_b, lru_wa, lru_ba, lru_wx, lru_bx,
                 lru_a_param, lru_w_out, fox_w_in, fox_b_f, fox_w_out):
    f = np.float32
    x = np.asarray(x, f)

    def col(v):
        return np.ascontiguousarray(np.asarray(v, f).reshape(-1, 128).T)

    gam = np.concatenate([col(norm_g[0]), col(norm_g[1]), col(final_g)], axis=1)
    cwv = np.asarray(lru_conv_w[0], f)
    cw = np.concatenate([col(cwv[k]) for k in range(4)], axis=1)
    vec = np.concatenate([col(lru_conv_b[0]), col(lru_ba[0]), col(lru_bx[0]), col(lru_a_param[0])], axis=1)
    ident = np.eye(128, dtype=f)
    kk = np.arange(128)[:, None]
    cc = np.arange(128)[None, :]
    tri = np.where(kk <= cc, 0.0, NEG).astype(f)
    common = {
        "gam": np.ascontiguousarray(gam), "w_in0": np.ascontiguousarray(lru_w_in[0], f), "cw": np.ascontiguousarray(cw),
        "vec": np.ascontiguousarray(vec), "wa": np.ascontiguousarray(lru_wa[0], f), "wx": np.ascontiguousarray(lru_wx[0], f),
        "w_out0": np.ascontiguousarray(lru_w_out[0], f), "w_in1": np.ascontiguousarray(fox_w_in[0], f),
        "bfv": np.ascontiguousarray(np.asarray(fox_b_f[0], f).reshape(16, 1)),
        "w_out1": np.ascontiguousarray(fox_w_out[0], f), "ident": ident, "tri": tri,
    }
    maps = []
    for b in range(2):
        xbT = np.ascontiguousarray(x[b].T)
        for j in range(4):
            P = 128 * (3 - j)
            xT = np.zeros((D, S), f)
            xT[:, P:] = xbT[:, :S - P]
            tokmask = np.ones((128, 512), f)
            tokmask[:, :P] = 0.0
            lsub = np.zeros((16, 512), f)
            if P > 0:
                lsub[:, 0] = NEG
                lsub[:, P] = -NEG
            m = dict(common)
            m.update({"xT": xT, "tokmask": tokmask, "lsub": lsub})
            maps.append(m)
    return maps


_NC_CACHE = {}


def kernel(**inputs):
    maps = make_in_maps(**inputs)
    if "nc" not in _NC_CACHE:
        _NC_CACHE["nc"] = build()
    nc = _NC_CACHE["nc"]
    res = run_bass_kernel_spmd(nc, maps, core_ids=list(range(8)))
    outp = np.zeros((2, S, D), np.float32)
    for b in range(2):
        for j in range(4):
            o = np.asarray(res.results[b * 4 + j]["out"])
            for s in range(16):
                t = 128 * (4 * s + j)
                outp[b, t:t + 128, :] = o[:, s * 128:(s + 1) * 128].T
    return outp
```

```python
import contextlib
import numpy as np
import concourse.bass as bass
import concourse.mybir as mybir
from concourse.bass_utils import run_bass_kernel_spmd

F32 = mybir.dt.float32
BF16 = mybir.dt.bfloat16
AF = mybir.ActivationFunctionType
ALU = mybir.AluOpType

D = 1024
S = 8192
W = 1536
EPS = 1e-6
NEG = -30000.0


class Buf:
    __slots__ = ("name", "last_w", "readers", "dma_readers", "sem", "cnt")

    def __init__(self, name):
        self.name = name
        self.last_w = None
        self.readers = {}
        self.dma_readers = []
        self.sem = None
        self.cnt = 0


class Op:
    __slots__ = ("eng", "fn", "deps", "sem", "val", "is_dma", "needs_inc")

    def __init__(self, eng, fn, is_dma=False):
        self.eng = eng
        self.fn = fn
        self.deps = []
        self.sem = None
        self.val = 0
        self.is_dma = is_dma
        self.needs_inc = False


ENGS = ("pe", "act", "dve", "pool", "sp")
SEM_ROT = 30000


class Sched:
    def __init__(self, nc, stack):
        self.nc = nc
        self.stack = stack
        self.ops = []
        self.eng_sem = {}
        self.eng_cnt = {e: 0 for e in ENGS}
        self.sem_final = {}
        self.waited = {e: {} for e in ENGS}
        self.nsem = 0
        for e in ENGS:
            self._new_eng_sem(e)

    def _alloc_sem(self, name):
        self.nsem += 1
        return self.stack.enter_context(self.nc.semaphore(f"{name}_{self.nsem}"))

    def _new_eng_sem(self, e):
        self.eng_sem[e] = self._alloc_sem("e" + e)
        self.eng_cnt[e] = 0

    def _track(self, op, reads, writes):
        deps = []
        seen = set()

        def add(d):
            if d is not None and id(d) not in seen and d is not op:
                seen.add(id(d))
                deps.append(d)

        for b in list(reads) + list(writes):
            add(b.last_w)
        for b in writes:
            for r in b.readers.values():
                add(r)
            for r in b.dma_readers:
                add(r)
        wset = set(id(b) for b in writes)
        for b in writes:
            b.last_w = op
            b.readers = {}
            b.dma_readers = []
        for b in reads:
            if id(b) in wset:
                continue
            if op.is_dma:
                b.dma_readers.append(op)
            else:
                b.readers[op.eng] = op
        for d in deps:
            if d.is_dma:
                continue
            if d.eng == op.eng and d.eng == "pe" and not op.is_dma:
                continue
            d.needs_inc = True
        op.deps = deps

    def op(self, eng, fn, reads=(), writes=()):
        o = Op(eng, fn)
        self._track(o, reads, writes)
        self.ops.append(o)
        return o

    def dma(self, fn, reads=(), writes=(), sem_buf=None, queue="sp"):
        o = Op(queue, fn, is_dma=True)
        self._track(o, reads, writes)
        if sem_buf.sem is None:
            sem_buf.sem = self._alloc_sem("d")
            sem_buf.cnt = 0
        sem_buf.cnt += 1
        o.sem = sem_buf.sem
        o.val = 16 * sem_buf.cnt
        self.sem_final[id(o.sem)] = (o.sem, o.val)
        self.ops.append(o)
        return o

    def flush(self):
        nc = self.nc
        ops = self.ops
        self.ops = []
        for e in ENGS:
            for o in reversed(ops):
                if o.eng == e and not o.is_dma:
                    o.needs_inc = True
                    break
        for o in ops:
            if o.is_dma:
                continue
            if o.needs_inc:
                if self.eng_cnt[o.eng] >= SEM_ROT:
                    self._new_eng_sem(o.eng)
                self.eng_cnt[o.eng] += 1
                o.sem = self.eng_sem[o.eng]
                o.val = self.eng_cnt[o.eng]
                self.sem_final[id(o.sem)] = (o.sem, o.val)
        finals = list(self.sem_final.values())
        per_eng = {e: [o for o in ops if o.eng == e] for e in ENGS}

        def emit(e, eng):
            waited = self.waited[e]
            for o in per_eng[e]:
                for d in o.deps:
                    if d.sem is None:
                        continue
                    if d.eng == e and e == "pe" and not d.is_dma and not o.is_dma:
                        continue
                    k = id(d.sem)
                    if waited.get(k, 0) < d.val:
                        eng.wait_ge(d.sem, d.val)
                        waited[k] = d.val
                ins = o.fn(eng)
                if o.is_dma:
                    ins.then_inc(o.sem, 16)
                elif o.needs_inc:
                    ins.then_inc(o.sem, 1)
            for (s, v) in finals:
                k = id(s)
                if waited.get(k, 0) < v:
                    eng.wait_ge(s, v)
                    waited[k] = v

        with nc.Block() as block:
            @block.tensor
            def _(eng):
                emit("pe", eng)

            @block.scalar
            def _(eng):
                emit("act", eng)

            @block.vector
            def _(eng):
                emit("dve", eng)

            @block.gpsimd
            def _(eng):
                emit("pool", eng)

            @block.sync
            def _(eng):
                emit("sp", eng)


class T:
    def __init__(self, t, name, n=1):
        self.t = t
        self.bs = [Buf(f"{name}_{i}") for i in range(n)]
        self.b = self.bs[0]


def build(phases=3, dbg=False):
    nc = bass.Bass("TRN2", target_bir_lowering=False)

    def din(name, shape):
        return nc.dram_tensor(name, shape, F32, kind="ExternalInput").ap()

    xT = din("xT", [D, S])
    gam = din("gam", [128, 24])
    w_in0 = din("w_in0", [D, 3072])
    cw = din("cw", [128, 48])
    vec = din("vec", [128, 48])
    wa = din("wa", [12, 128, 128])
    wx = din("wx", [12, 128, 128])
    w_out0 = din("w_out0", [W, D])
    w_in1 = din("w_in1", [D, 4112])
    bfv = din("bfv", [16, 1])
    w_out1 = din("w_out1", [D, D])
    ident_d = din("ident", [128, 128])
    tri_d = din("tri", [128, 128])
    tokmask_d = din("tokmask", [128, 512])
    lsub_d = din("lsub", [16, 512])
    out = nc.dram_tensor("out", [D, 2048], F32, kind="ExternalOutput").ap()
    skind = "ExternalOutput" if dbg else "Internal"
    ys = nc.dram_tensor("ys", [W, S], BF16, kind=skind).ap()
    x1o = nc.dram_tensor("x1o", [D, 2048], F32, kind=skind).ap()
    ks = nc.dram_tensor("ks", [D, S], BF16, kind=skind).ap()
    vs = nc.dram_tensor("vs", [16, 128, 64, 128], BF16, kind=skind).ap()
    cpos = nc.dram_tensor("cpos", [16, 3, S], BF16, kind=skind).ap()
    cneg = nc.dram_tensor("cneg", [16, 3, S], BF16, kind=skind).ap()

    with contextlib.ExitStack() as gst:
        S_ = Sched(nc, gst)

        with contextlib.ExitStack() as st:
            def sb(name, shape, dt, n=1):
                return T(st.enter_context(nc.sbuf_tensor(name, shape, dt)), name, n)

            TT = 256
            NT1 = S // TT
            NCH = 12
            NI = NT1 * NCH
            w0 = sb("w0", [128, 8, 3072], BF16)
            wab = sb("wab", [128, 12, 128], BF16)
            wxb = sb("wxb", [128, 12, 128], BF16)
            dg = sb("dg", [128, 48, 128], BF16)
            gam_sb = sb("gam_sb", [128, 24], F32)
            vec_sb = sb("vec_sb", [128, 48], F32)
            onesb = sb("onesb", [128, 128], BF16)
            hc = sb("hc", [128, 12], F32)
            hb = sb("hb", [128, 24], F32)
            ctmp = sb("ctmp", [128, 12], F32)
            tokm = sb("tokm", [128, 512], BF16)
            X = [sb(f"X{i}", [128, 8, TT], F32) for i in range(2)]
            sq = sb("sq", [128, 8, TT], BF16, 8)
            xbf = [sb(f"xbf{i}", [128, 8, TT], BF16, 8) for i in range(2)]
            rt = sb("rt", [128, TT], F32)
            rstd = [sb(f"rstd{i}", [128, TT], F32) for i in range(2)]
            rstdh = [sb(f"rstdh{i}", [128, TT], F32) for i in range(2)]
            xb = sb("xb", [128, 12, TT + 3], BF16, 12)
            NXC, NXCB, NTHR, NTHI, NGH, NTHG = 5, 3, 2, 2, 3, 2
            xcr = [sb(f"xcr{i}", [128, TT], F32) for i in range(NXC)]
            xcbr = [sb(f"xcbr{i}", [128, TT], BF16) for i in range(NXCB)]
            thr = [sb(f"thr{i}", [128, TT], F32) for i in range(NTHR)]
            thi = [sb(f"thi{i}", [128, TT], F32) for i in range(NTHI)]
            ghr = [sb(f"ghr{i}", [128, TT], F32) for i in range(NGH)]
            thg = [sb(f"thg{i}", [128, TT], F32) for i in range(NTHG)]
            aa = [sb(f"aa{i}", [128, 12, TT], F32, 12) for i in range(2)]
            u1 = [sb(f"u1{i}", [128, 12, TT], F32, 12) for i in range(2)]
            sg = [sb(f"sg{i}", [128, 12, TT], BF16, 12) for i in range(2)]
            tmp = sb("tmp", [128, 12, TT], F32, 12)
            hprev = sb("hprev", [128, 12], F32, 12)
            yb = [sb(f"yb{i}", [128, 12, TT], BF16, 12) for i in range(2)]
            banks = [st.enter_context(nc.psum_tensor(f"pbA{i}", [128, 512], F32)) for i in range(8)]
            bank_bufs = [Buf(f"pbA{i}") for i in range(8)]

            class PSl:
                def __init__(self, bank, half):
                    self.bank = banks[bank]
                    self.c0 = half * TT
                    self.b = bank_bufs[bank]

                def ap(self):
                    return self.bank[:, self.c0:self.c0 + TT]

            NUG = 3
            psUs = [PSl(i, 0) for i in range(NUG)]
            psGs = [PSl(i, 1) for i in range(NUG)]
            psCs = [PSl(3, 0), PSl(4, 0)]
            psRs = [PSl(5, 0), PSl(6, 0)]
            psIs = [PSl(5, 1), PSl(6, 1)]
            psN = PSl(7, 0)

            with contextlib.ExitStack() as st0:
                identf = T(st0.enter_context(nc.sbuf_tensor("identf", [128, 128], F32)), "identf")
                cw_sb = T(st0.enter_context(nc.sbuf_tensor("cw_sb", [128, 48], F32)), "cw_sb")
                for (dst, src) in ((gam_sb, gam), (cw_sb, cw), (vec_sb, vec), (identf, ident_d)):
                    S_.dma(lambda e, d=dst, s=src: e.dma_start(out=d.t[:], in_=s[:, :]), writes=[dst.b], sem_buf=dst.b)
                for i in range(48):
                    S_.op("pool", lambda e, i=i: e.tensor_scalar(out=dg.t[:, i, :], in0=identf.t[:], scalar1=cw_sb.t[:, i:i + 1],
                                                                 scalar2=None, op0=ALU.mult),
                          reads=[identf.b, cw_sb.b], writes=[dg.b])
                S_.flush()
            S_.dma(lambda e: e.dma_start(out=tokm.t[:], in_=tokmask_d[:, :]), writes=[tokm.b], sem_buf=tokm.b, queue="pool")
            w0v = w_in0.rearrange("(k p) n -> p k n", p=128)
            stg = [aa[0], aa[1], u1[0], u1[1]]
            for k in range(8):
                sgt = stg[k % 4]
                S_.dma(lambda e, k=k, sgt=sgt: e.dma_start(out=sgt.t[:].rearrange("p c t -> p (c t)"), in_=w0v[:, k, :]),
                       writes=sgt.bs, sem_buf=sgt.bs[0])
                S_.op("dve" if k % 2 == 0 else "act",
                      (lambda e, k=k, sgt=sgt: e.tensor_scalar(out=w0.t[:, k, :], in0=sgt.t[:].rearrange("p c t -> p (c t)"),
                                                               scalar1=gam_sb.t[:, k:k + 1], scalar2=None, op0=ALU.mult)) if k % 2 == 0 else
                      (lambda e, k=k, sgt=sgt: e.activation(out=w0.t[:, k, :], in_=sgt.t[:].rearrange("p c t -> p (c t)"), func=AF.Copy,
                                                            scale=gam_sb.t[:, k:k + 1])),
                      reads=sgt.bs + [gam_sb.b], writes=[w0.b])
            S_.dma(lambda e: e.dma_start(out=wab.t[:], in_=wa.rearrange("n c d -> c n d")), writes=[wab.b], sem_buf=wab.b, queue="pool")
            S_.dma(lambda e: e.dma_start(out=wxb.t[:], in_=wx.rearrange("n c d -> c n d")), writes=[wxb.b], sem_buf=wxb.b, queue="pool")
            S_.op("pool", lambda e: e.memset(onesb.t[:], 1.0), writes=[onesb.b])
            S_.op("pool", lambda e: e.memset(hprev.t[:], 0.0), writes=hprev.bs)
            S_.op("pool", lambda e: e.memset(xb.t[:], 0.0), writes=xb.bs)
            S_.op("act", lambda e: e.activation(out=ctmp.t[:], in_=vec_sb.t[:, 36:48], func=AF.Exp, scale=-1.0),
                  reads=[vec_sb.b], writes=[ctmp.b])
            S_.op("act", lambda e: e.activation(out=ctmp.t[:], in_=ctmp.t[:], func=AF.Ln, bias=1.0),
                  reads=[ctmp.b], writes=[ctmp.b])
            S_.op("dve", lambda e: e.tensor_scalar(out=hc.t[:], in0=ctmp.t[:], scalar1=-4.0, scalar2=None, op0=ALU.mult),
                  reads=[ctmp.b], writes=[hc.b])
            S_.op("dve", lambda e: e.tensor_scalar(out=hb.t[:], in0=vec_sb.t[:, 12:36], scalar1=0.5, scalar2=None, op0=ALU.mult),
                  reads=[vec_sb.b], writes=[hb.b])

            xTv = xT.rearrange("(k p) t -> p k t", p=128)
            ysv = ys.rearrange("(c p) t -> p c t", p=128)

            def load_x(i):
                Xc = X[i % 2]
                S_.dma(lambda e, t0=i * TT, Xc=Xc: e.dma_start(out=Xc.t[:], in_=xTv[:, :, t0:t0 + TT]), writes=[Xc.b], sem_buf=Xc.b)

            def casts(i):
                xbc = xbf[i % 2]
                S_.dma(lambda e, t0=i * TT, xbc=xbc: e.dma_start(out=xbc.t[:], in_=xTv[:, :, t0:t0 + TT]), writes=xbc.bs,
                       sem_buf=xbc.bs[0], queue="pool")

            def sq_chunk(i, k):
                Xc = X[i % 2]
                S_.op("pool", lambda e, Xc=Xc, k=k: e.tensor_tensor(out=sq.t[:, k, :], in0=Xc.t[:, k, :], in1=Xc.t[:, k, :], op=ALU.mult),
                      reads=[Xc.b], writes=[sq.bs[k]])

            def norm_mm(i):
                def mmN(e):
                    for k in range(8):
                        ins = e.matmul(psN.ap(), lhsT=onesb.t[:], rhs=sq.t[:, k, :], start=(k == 0), stop=(k == 7))
                    return ins
                S_.op("pe", mmN, reads=[onesb.b] + sq.bs, writes=[psN.b])
                S_.op("dve", lambda e: e.tensor_scalar(out=rt.t[:], in0=psN.ap(), scalar1=1.0 / D, scalar2=EPS,
                                                       op0=ALU.mult, op1=ALU.add), reads=[psN.b], writes=[rt.b])

            def norm_sqrt():
                S_.op("act", lambda e: e.activation(out=rt.t[:], in_=rt.t[:], func=AF.Sqrt), reads=[rt.b], writes=[rt.b])

            def norm_b(i):
                r_, rh_ = rstd[i % 2], rstdh[i % 2]
                S_.op("dve", lambda e, r_=r_: e.reciprocal(out=r_.t[:], in_=rt.t[:]), reads=[rt.b], writes=[r_.b])
                S_.op("dve", lambda e, r_=r_, rh_=rh_: e.tensor_scalar(out=rh_.t[:], in0=r_.t[:], scalar1=0.5, scalar2=None, op0=ALU.mult),
                      reads=[r_.b], writes=[rh_.b])

            def op_UG(it, m):
                xbc = xbf[it % 2]
                n = it * NCH + m
                pu, pg = psUs[n % NUG], psGs[n % NUG]

                def mmU(e):
                    for k in range(8):
                        ins = e.matmul(pu.ap(), lhsT=w0.t[:, k, m * 128:(m + 1) * 128], rhs=xbc.t[:, k, :], start=(k == 0), stop=(k == 7))
                    return ins
                S_.op("pe", mmU, reads=[w0.b] + xbc.bs, writes=[pu.b])

                def mmG(e):
                    for k in range(8):
                        ins = e.matmul(pg.ap(), lhsT=w0.t[:, k, W + m * 128:W + (m + 1) * 128], rhs=xbc.t[:, k, :], start=(k == 0), stop=(k == 7))
                    return ins
                S_.op("pe", mmG, reads=[w0.b] + xbc.bs, writes=[pg.b])

            def op_EV(it, m):
                n = it * NCH + m
                pu, pg = psUs[n % NUG], psGs[n % NUG]
                r_, rh_ = rstd[it % 2], rstdh[it % 2]
                g_ = ghr[n % NGH]
                S_.op("dve", lambda e: e.tensor_tensor(out=xb.t[:, m, 3:3 + TT], in0=pu.ap(), in1=r_.t[:], op=ALU.mult),
                      reads=[pu.b, r_.b], writes=[xb.bs[m]])
                S_.op("dve", lambda e: e.tensor_tensor(out=g_.t[:], in0=pg.ap(), in1=rh_.t[:], op=ALU.mult),
                      reads=[pg.b, rh_.b], writes=[g_.b])

            def op_C(it, m):
                n = it * NCH + m
                pc = psCs[n % 2]

                def mmC(e):
                    for k in range(4):
                        ins = e.matmul(pc.ap(), lhsT=dg.t[:, k * 12 + m, :], rhs=xb.t[:, m, k:k + TT], start=(k == 0), stop=(k == 3))
                    return ins
                S_.op("pe", mmC, reads=[dg.b, xb.bs[m]], writes=[pc.b])

            def op_THG(it, m):
                n = it * NCH + m
                g_, tg_ = ghr[n % NGH], thg[n % NTHG]
                S_.op("act", lambda e: e.activation(out=tg_.t[:], in_=g_.t[:], func=AF.Tanh), reads=[g_.b], writes=[tg_.b])

            def op_SG(it, m):
                n = it * NCH + m
                g_, tg_, sgc = ghr[n % NGH], thg[n % NTHG], sg[it % 2]
                S_.op("dve", lambda e: e.scalar_tensor_tensor(out=sgc.t[:, m, :], in0=tg_.t[:], scalar=1.0, in1=g_.t[:],
                                                              op0=ALU.add, op1=ALU.mult),
                      reads=[g_.b, tg_.b], writes=[sgc.bs[m]])

            def op_XC(it, m):
                n = it * NCH + m
                pc, xc_ = psCs[n % 2], xcr[n % NXC]
                S_.op("act", lambda e: e.activation(out=xc_.t[:], in_=pc.ap(), func=AF.Identity, bias=vec_sb.t[:, m:m + 1]),
                      reads=[pc.b, vec_sb.b], writes=[xc_.b])

            def op_HALO(it, m):
                S_.op("pool", lambda e: e.tensor_copy(out=xb.t[:, m, 0:3], in_=xb.t[:, m, TT:TT + 3]),
                      reads=[xb.bs[m]], writes=[xb.bs[m]])

            def op_XCB(it, m):
                n = it * NCH + m
                pc, xcb_ = psCs[n % 2], xcbr[n % NXCB]
                S_.op("dve", lambda e: e.tensor_scalar(out=xcb_.t[:], in0=pc.ap(), scalar1=vec_sb.t[:, m:m + 1], scalar2=None, op0=ALU.add),
                      reads=[vec_sb.b], writes=[xcb_.b, pc.b])

            def op_RI(it, m):
                n = it * NCH + m
                pr, pi, xcb_ = psRs[n % 2], psIs[n % 2], xcbr[n % NXCB]
                S_.op("pe", lambda e: e.matmul(pr.ap(), lhsT=wab.t[:, m, :], rhs=xcb_.t[:], start=True, stop=True),
                      reads=[wab.b, xcb_.b], writes=[pr.b])
                S_.op("pe", lambda e: e.matmul(pi.ap(), lhsT=wxb.t[:, m, :], rhs=xcb_.t[:], start=True, stop=True),
                      reads=[wxb.b, xcb_.b], writes=[pi.b])

            def op_TH(it, m):
                n = it * NCH + m
                pr, pi, tr_, ti_ = psRs[n % 2], psIs[n % 2], thr[n % NTHR], thi[n % NTHI]
                S_.op("act", lambda e: e.activation(out=tr_.t[:], in_=pr.ap(), func=AF.Tanh, bias=hb.t[:, m:m + 1], scale=0.5),
                      reads=[pr.b, hb.b], writes=[tr_.b])
                S_.op("act", lambda e: e.activation(out=ti_.t[:], in_=pi.ap(), func=AF.Tanh, bias=hb.t[:, 12 + m:13 + m], scale=0.5),
                      reads=[pi.b, hb.b], writes=[ti_.b])

            def op_AA(it, m):
                n = it * NCH + m
                tr_, aac = thr[n % NTHR], aa[it % 2]
                S_.op("act", lambda e: e.activation(out=aac.t[:, m, :], in_=tr_.t[:], func=AF.Exp, scale=hc.t[:, m:m + 1], bias=hc.t[:, m:m + 1]),
                      reads=[tr_.b, hc.b], writes=[aac.bs[m]])

            def op_U1(it, m):
                n = it * NCH + m
                xc_, ti_, u1c = xcr[n % NXC], thi[n % NTHI], u1[it % 2]
                S_.op("dve", lambda e: e.scalar_tensor_tensor(out=u1c.t[:, m, :], in0=ti_.t[:], scalar=1.0, in1=xc_.t[:],
                                                              op0=ALU.add, op1=ALU.mult),
                      reads=[xc_.b, ti_.b], writes=[u1c.bs[m]])

            def op_MID(it, m):
                u1p, aap, sgp, ybc = u1[it % 2], aa[it % 2], sg[it % 2], yb[it % 2]
                t0p = it * TT
                S_.op("pool", lambda e: e.tensor_tensor(out=u1p.t[:, m, :], in0=u1p.t[:, m, :], in1=tmp.t[:, m, :], op=ALU.mult),
                      reads=[u1p.bs[m], tmp.bs[m]], writes=[u1p.bs[m]])
                if t0p < 512:
                    S_.op("pool", lambda e: e.tensor_tensor(out=u1p.t[:, m, :], in0=u1p.t[:, m, :], in1=tokm.t[:, t0p:t0p + TT], op=ALU.mult),
                          reads=[u1p.bs[m], tokm.b], writes=[u1p.bs[m]])
                S_.op("dve", lambda e: e.tensor_tensor_scan(out=tmp.t[:, m, :], data0=aap.t[:, m, :], data1=u1p.t[:, m, :],
                                                            initial=hprev.t[:, m:m + 1], op0=ALU.mult, op1=ALU.add),
                      reads=[aap.bs[m], u1p.bs[m], hprev.bs[m], tmp.bs[m]], writes=[tmp.bs[m]])
                S_.op("pool", lambda e: e.tensor_copy(out=hprev.t[:, m:m + 1], in_=tmp.t[:, m, TT - 1:TT]),
                      reads=[tmp.bs[m]], writes=[hprev.bs[m]])
                S_.op("pool", lambda e: e.tensor_tensor(out=ybc.t[:, m, :], in0=tmp.t[:, m, :], in1=sgp.t[:, m, :], op=ALU.mult),
                      reads=[tmp.bs[m], sgp.bs[m]], writes=[ybc.bs[m]])
                if m == NCH - 1:
                    S_.dma(lambda e: e.dma_start(out=ysv[:, :, t0p:t0p + TT], in_=ybc.t[:]), reads=ybc.bs, sem_buf=ybc.bs[0])

            def batch(it):
                aap = aa[it % 2]
                S_.op("act", lambda e: e.activation(out=tmp.t[:], in_=aap.t[:], func=AF.Square), reads=aap.bs, writes=tmp.bs)
                S_.op("act", lambda e: e.activation(out=tmp.t[:], in_=tmp.t[:], func=AF.Sqrt, scale=-0.25, bias=0.25),
                      reads=tmp.bs, writes=tmp.bs)

            LAGS = [(0, op_UG), (1, op_EV), (2, op_C), (2, op_THG), (3, op_SG), (3, op_XC), (3, op_HALO), (3, op_XCB),
                    (4, op_RI), (5, op_TH), (6, op_AA), (6, op_U1), (19, op_MID)]
            load_x(0)
            load_x(1)
            casts(0)
            for k in range(8):
                sq_chunk(0, k)
            norm_mm(0)
            norm_sqrt()
            norm_b(0)
            load_x(2)
            for k in range(4):
                sq_chunk(1, k)
            for g in range(NI + 20):
                for (L, fn) in LAGS:
                    n = g - L
                    if 0 <= n < NI:
                        fn(n // NCH, n % NCH)
                t, r = divmod(g, NCH)
                if r >= 8 and t + 2 < NT1:
                    sq_chunk(t + 2, r - 8)
                if r <= 3 and t + 1 < NT1:
                    sq_chunk(t + 1, r + 4)
                if r == 4 and t + 1 < NT1:
                    norm_mm(t + 1)
                if r == 6:
                    if t + 1 < NT1:
                        norm_sqrt()
                    if 0 <= t - 1 < NT1:
                        batch(t - 1)
                    if t + 1 < NT1:
                        norm_b(t + 1)
                if r == 9 and t + 1 < NT1:
                    casts(t + 1)
                    if t + 3 < NT1:
                        load_x(t + 3)
            S_.flush()

        if phases < 2:
            return nc
        with contextlib.ExitStack() as st:
            def sb(name, shape, dt, n=1):
                return T(st.enter_context(nc.sbuf_tensor(name, shape, dt)), name, n)

            def ps(name):
                return T(st.enter_context(nc.psum_tensor(name, [128, 512], F32)), name)

            TT = 512
            NT2 = S // TT
            wo0 = sb("wo0", [128, 12, 1024], BF16)
            wk = sb("wk", [128, 8, 1024], BF16)
            wv = sb("wv", [128, 8, 1024], BF16)
            wf = sb("wf", [128, 8, 16], BF16)
            gam_sb = sb("gam_sb2", [128, 24], F32)
            onesb = sb("onesb2", [128, 128], BF16)
            ones16 = sb("ones16", [16, 512], F32)
            nbf = sb("nbf", [16, 1], F32)
            lsub = sb("lsub_sb", [16, 512], F32)
            X = [sb(f"X2_{i}", [128, 8, TT], F32, 8) for i in range(2)]
            Y = [sb(f"Y2_{i}", [128, 12, TT], BF16) for i in range(2)]
            sq = [sb(f"sq2_{i}", [128, 8, TT], BF16, 8) for i in range(2)]
            xbf = [sb(f"xbf2_{i}", [128, 8, TT], BF16, 8) for i in range(2)]
            rt = sb("rt2", [128, TT], F32)
            rstd = [sb(f"rstd2_{i}", [128, TT], F32) for i in range(2)]
            rtT = sb("rtT", [128, 4], F32)
            rstdT = [sb(f"rstdT{i}", [128, 4], F32) for i in range(2)]
            kst = sb("kst", [128, 8, TT], BF16)
            vst = sb("vst", [128, 8, 2, 4, 128], BF16)
            zf = sb("zf", [16, 512], F32)
            cc = sb("cc", [16, 512], F32)
            r1 = sb("r1", [16, 512], F32)
            cprev = sb("cprev", [16, 1], F32)
            Pp = sb("Pp", [16, 3, 512], BF16)
            Pn = sb("Pn", [16, 3, 512], BF16)
            psO = [ps("psO0"), ps("psO1")]
            psN = ps("psN2")
            psT = ps("psT2")
            psK = [ps("psK0"), ps("psK1")]
            psV = [ps("psV0"), ps("psV1")]

            w1v = w_in1.rearrange("(k p) n -> p k n", p=128)
            wo0v = w_out0.rearrange("(k p) n -> p k n", p=128)
            for k in range(12):
                S_.dma(lambda e, k=k: e.dma_start(out=wo0.t[:, k, :], in_=wo0v[:, k, :]), writes=[wo0.b], sem_buf=wo0.b, queue="pool")
            for k in range(8):
                S_.dma(lambda e, k=k: e.dma_start(out=wk.t[:, k, :], in_=w1v[:, k, 1024:2048]), writes=[wk.b], sem_buf=wk.b, queue="pool")
                S_.dma(lambda e, k=k: e.dma_start(out=wv.t[:, k, :], in_=w1v[:, k, 2048:3072]), writes=[wv.b], sem_buf=wv.b, queue="pool")
            S_.dma(lambda e: e.dma_start(out=wf.t[:], in_=w1v[:, :, 4096:4112]), writes=[wf.b], sem_buf=wf.b, queue="pool")
            S_.dma(lambda e: e.dma_start(out=gam_sb.t[:], in_=gam[:, :]), writes=[gam_sb.b], sem_buf=gam_sb.b)
            S_.dma(lambda e: e.dma_start(out=nbf.t[:], in_=bfv[:, :]), writes=[nbf.b], sem_buf=nbf.b)
            S_.dma(lambda e: e.dma_start(out=lsub.t[:], in_=lsub_d[:, :]), writes=[lsub.b], sem_buf=lsub.b)
            S_.op("dve", lambda e: e.tensor_scalar(out=nbf.t[:], in0=nbf.t[:], scalar1=-1.0, scalar2=None, op0=ALU.mult),
                  reads=[nbf.b], writes=[nbf.b])
            S_.op("pool", lambda e: e.memset(onesb.t[:], 1.0), writes=[onesb.b])
            S_.op("pool", lambda e: e.memset(ones16.t[:], 1.0), writes=[ones16.b])
            S_.op("pool", lambda e: e.memset(cprev.t[:], 0.0), writes=[cprev.b])
            S_.op("pool", lambda e: e.memset(vst.t[:], 1.0), writes=[vst.b])

            xTv = xT.rearrange("(k p) t -> p k t", p=128)
            ysv = ys.rearrange("(c p) t -> p c t", p=128)
            x1ov = x1o.rearrange("(k p) t -> p k t", p=128)
            ksv = ks.rearrange("(m p) t -> p m t", p=128)
            vsv = vs.rearrange("(g e) p kb c -> p g e kb c", e=2)

            def load2(i):
                Xc, Yc = X[i % 2], Y[i % 2]
                S_.dma(lambda e, t0=i * TT, Xc=Xc: e.dma_start(out=Xc.t[:], in_=xTv[:, :, t0:t0 + TT]), writes=Xc.bs, sem_buf=Xc.bs[0])
                S_.dma(lambda e, t0=i * TT, Yc=Yc: e.dma_start(out=Yc.t[:], in_=ysv[:, :, t0:t0 + TT]), writes=[Yc.b], sem_buf=Yc.b)

            def stage_o(i):
                p_ = i % 2
                Xc, Yc, sqc, xbc = X[p_], Y[p_], sq[p_], xbf[p_]
                for mo in range(8):
                    po = psO[mo % 2]

                    def mmO(e, mo=mo, po=po, Yc=Yc):
                        for c in range(12):
                            ins = e.matmul(po.t[:, :], lhsT=wo0.t[:, c, mo * 128:(mo + 1) * 128], rhs=Yc.t[:, c, :],
                                           start=(c == 0), stop=(c == 11))
                        return ins
                    S_.op("pe", mmO, reads=[wo0.b, Yc.b], writes=[po.b])
                    S_.op("dve", lambda e, mo=mo, po=po, Xc=Xc: e.tensor_tensor(out=Xc.t[:, mo, :], in0=po.t[:, :], in1=Xc.t[:, mo, :], op=ALU.add),
                          reads=[po.b, Xc.bs[mo]], writes=[Xc.bs[mo]])
                    S_.op("pool", lambda e, mo=mo, Xc=Xc, sqc=sqc: e.tensor_tensor(out=sqc.t[:, mo, :], in0=Xc.t[:, mo, :], in1=Xc.t[:, mo, :], op=ALU.mult),
                          reads=[Xc.bs[mo]], writes=[sqc.bs[mo]])
                    S_.op("act", lambda e, mo=mo, Xc=Xc, xbc=xbc: e.activation(out=xbc.t[:, mo, :], in_=Xc.t[:, mo, :], func=AF.Copy,
                                                                              scale=gam_sb.t[:, 8 + mo:9 + mo]),
                          reads=[Xc.bs[mo], gam_sb.b], writes=[xbc.bs[mo]])
                S_.dma(lambda e, i=i, Xc=Xc: e.dma_start(out=x1ov[:, :, i * 128:(i + 1) * 128], in_=Xc.t[:, :, 384:512]),
                       reads=Xc.bs, sem_buf=Xc.bs[1])

                def mmN(e, sqc=sqc):
                    for k in range(8):
                        ins = e.matmul(psN.t[:, :], lhsT=onesb.t[:], rhs=sqc.t[:, k, :], start=(k == 0), stop=(k == 7))
                    return ins
                S_.op("pe", mmN, reads=[onesb.b] + sqc.bs, writes=[psN.b])

                def mmT(e, sqc=sqc):
                    for j in range(4):
                        for k in range(8):
                            ins = e.matmul(psT.t[:, j:j + 1], lhsT=sqc.t[:, k, j * 128:(j + 1) * 128], rhs=onesb.t[:, 0:1],
                                           start=(k == 0), stop=(k == 7))
                    return ins
                S_.op("pe", mmT, reads=[onesb.b] + sqc.bs, writes=[psT.b])
                S_.op("dve", lambda e: e.tensor_scalar(out=rt.t[:], in0=psN.t[:, :], scalar1=1.0 / D, scalar2=EPS,
                                                       op0=ALU.mult, op1=ALU.add), reads=[psN.b], writes=[rt.b])
                S_.op("dve", lambda e: e.tensor_scalar(out=rtT.t[:], in0=psT.t[:, 0:4], scalar1=1.0 / D, scalar2=EPS,
                                                       op0=ALU.mult, op1=ALU.add), reads=[psT.b], writes=[rtT.b])
                S_.op("act", lambda e: e.activation(out=rt.t[:], in_=rt.t[:], func=AF.Sqrt), reads=[rt.b], writes=[rt.b])
                S_.op("act", lambda e: e.activation(out=rtT.t[:], in_=rtT.t[:], func=AF.Sqrt), reads=[rtT.b], writes=[rtT.b])
                S_.op("dve", lambda e, r_=rstd[p_]: e.reciprocal(out=r_.t[:], in_=rt.t[:]), reads=[rt.b], writes=[rstd[p_].b])
                S_.op("dve", lambda e, r_=rstdT[p_]: e.reciprocal(out=r_.t[:], in_=rtT.t[:]), reads=[rtT.b], writes=[rstdT[p_].b])

            def stage_kv(i):
                p_ = i % 2
                t0 = i * TT
                xbc, r_, rT_ = xbf[p_], rstd[p_], rstdT[p_]
                for m in range(8):
                    pk = psK[m % 2]

                    def mmK(e, m=m, pk=pk):
                        for k in range(8):
                            ins = e.matmul(pk.t[:, :], lhsT=wk.t[:, k, m * 128:(m + 1) * 128], rhs=xbc.t[:, k, :],
                                           start=(k == 0), stop=(k == 7))
                        return ins
                    S_.op("pe", mmK, reads=[wk.b] + xbc.bs, writes=[pk.b])
                    S_.op("dve", lambda e, m=m, pk=pk: e.tensor_tensor(out=kst.t[:, m, :], in0=pk.t[:, :], in1=r_.t[:], op=ALU.mult),
                          reads=[pk.b, r_.b], writes=[kst.b])
                S_.dma(lambda e, t0=t0: e.dma_start(out=ksv[:, :, t0:t0 + TT], in_=kst.t[:]), reads=[kst.b], sem_buf=kst.b)
                for j in range(4):
                    for half in range(2):
                        pv = psV[(j * 2 + half) % 2]

                        def mmV(e, j=j, half=half, pv=pv):
                            for k in range(8):
                                ins = e.matmul(pv.t[:, :], lhsT=xbc.t[:, k, j * 128:(j + 1) * 128],
                                               rhs=wv.t[:, k, half * 512:(half + 1) * 512], start=(k == 0), stop=(k == 7))
                            return ins
                        S_.op("pe", mmV, reads=[wv.b] + xbc.bs, writes=[pv.b])
                        pvv = pv.t[:, :].rearrange("p (g e d) -> p g e d", g=4, e=2, d=64)
                        S_.op("act", lambda e, j=j, half=half, pvv=pvv: e.activation(
                            out=vst.t[:, half * 4:(half + 1) * 4, 0, j, 0:64], in_=pvv[:, :, 0, :], func=AF.Copy,
                            scale=rT_.t[:, j:j + 1]), reads=[pv.b, rT_.b], writes=[vst.b])
                        S_.op("act", lambda e, j=j, half=half, pvv=pvv: e.activation(
                            out=vst.t[:, half * 4:(half + 1) * 4, 1, j, 64:128], in_=pvv[:, :, 1, :], func=AF.Copy,
                            scale=rT_.t[:, j:j + 1]), reads=[pv.b, rT_.b], writes=[vst.b])
                S_.dma(lambda e, i=i: e.dma_start(out=vsv[:, :, :, 4 * i:4 * i + 4, :], in_=vst.t[:]), reads=[vst.b], sem_buf=vst.b)

                def mmF(e):
                    for k in range(8):
                        ins = e.matmul(psT.t[0:16, :], lhsT=wf.t[:, k, :], rhs=xbc.t[:, k, :], start=(k == 0), stop=(k == 7))
                    return ins
                S_.op("pe", mmF, reads=[wf.b] + xbc.bs, writes=[psT.b])
                S_.op("dve", lambda e: e.tensor_tensor(out=zf.t[:], in0=psT.t[0:16, :], in1=r_.t[0:16, :], op=ALU.mult),
                      reads=[psT.b, r_.b], writes=[zf.b])
                S_.op("act", lambda e: e.activation(out=zf.t[:], in_=zf.t[:], func=AF.Exp, scale=-1.0, bias=nbf.t[:, 0:1]),
                      reads=[zf.b, nbf.b], writes=[zf.b])
                S_.op("act", lambda e: e.activation(out=zf.t[:], in_=zf.t[:], func=AF.Ln, bias=1.0), reads=[zf.b], writes=[zf.b])
                if i == 0:
                    S_.op("dve", lambda e: e.tensor_tensor(out=zf.t[:], in0=zf.t[:], in1=lsub.t[:], op=ALU.add),
                          reads=[zf.b, lsub.b], writes=[zf.b])
                S_.op("dve", lambda e: e.tensor_tensor_scan(out=cc.t[:], data0=ones16.t[:], data1=zf.t[:], initial=cprev.t[:, 0:1],
                                                            op0=ALU.mult, op1=ALU.subtract),
                      reads=[ones16.b, zf.b, cprev.b], writes=[cc.b])
                S_.op("dve", lambda e: e.tensor_copy(out=cprev.t[:], in_=cc.t[:, TT - 1:TT]), reads=[cc.b], writes=[cprev.b])
                S_.op("dve", lambda e: e.tensor_copy(out=Pp.t[:, 0, :], in_=cc.t[:]), reads=[cc.b], writes=[Pp.b])
                S_.op("dve", lambda e: e.tensor_tensor(out=r1.t[:], in0=cc.t[:], in1=Pp.t[:, 0, :], op=ALU.subtract),
                      reads=[cc.b, Pp.b], writes=[r1.b])
                S_.op("dve", lambda e: e.tensor_copy(out=Pp.t[:, 1, :], in_=r1.t[:]), reads=[r1.b], writes=[Pp.b])
                S_.op("dve", lambda e: e.tensor_tensor(out=r1.t[:], in0=r1.t[:], in1=Pp.t[:, 1, :], op=ALU.subtract),
                      reads=[r1.b, Pp.b], writes=[r1.b])
                S_.op("dve", lambda e: e.tensor_copy(out=Pp.t[:, 2, :], in_=r1.t[:]), reads=[r1.b], writes=[Pp.b])
                S_.op("dve", lambda e: e.tensor_scalar(out=Pn.t[:], in0=Pp.t[:], scalar1=-1.0, scalar2=None, op0=ALU.mult),
                      reads=[Pp.b], writes=[Pn.b])
                S_.dma(lambda e, t0=t0: e.dma_start(out=cpos[:, :, t0:t0 + TT], in_=Pp.t[:]), reads=[Pp.b], sem_buf=Pp.b)
                S_.dma(lambda e, t0=t0: e.dma_start(out=cneg[:, :, t0:t0 + TT], in_=Pn.t[:]), reads=[Pn.b], sem_buf=Pn.b)

            load2(0)
            for it in range(NT2 + 1):
                if it + 1 < NT2:
                    load2(it + 1)
                if it < NT2:
                    stage_o(it)
                if it >= 1:
                    stage_kv(it - 1)
            S_.flush()

        if phases < 3:
            return nc
        with contextlib.ExitStack() as st:
            def sb(name, shape, dt):
                return T(st.enter_context(nc.sbuf_tensor(name, shape, dt)), name)

            def ps(name):
                return T(st.enter_context(nc.psum_tensor(name, [128, 512], F32)), name)

            TT = 512
            wq = sb("wq", [128, 8, 1024], BF16)
            wg = sb("wg", [128, 8, 1024], BF16)
            wo1 = sb("wo1", [128, 8, 1024], BF16)
            gam_sb = sb("gam_sb3", [128, 24], F32)
            onesb = sb("onesb3", [128, 128], BF16)
            identb = sb("identb", [128, 128], BF16)
            trib = sb("trib", [128, 128], BF16)
            X = sb("X3", [128, 8, TT], F32)
            sq = sb("sq3", [128, 8, TT], BF16)
            xbf = sb("xbf3", [128, 8, TT], BF16)
            rt = sb("rt3", [128, TT], F32)
            rstd = sb("rstd3", [128, TT], F32)
            rq = sb("rq3", [128, TT], F32)
            tq = [sb(f"tq{i}", [128, TT], F32) for i in range(2)]
            qa = sb("qa", [70, 16, TT], BF16)
            sg = sb("sg3", [128, 8, TT], BF16)
            yh = sb("yh", [128, 8, TT], BF16)
            ka = [sb(f"ka{i}", [70, S], BF16) for i in range(2)]
            va = [sb(f"va{i}", [128, 64, 128], BF16) for i in range(2)]
            pt = [sb(f"pt{i}", [128, TT], BF16) for i in range(5)]
            rl = sb("rl", [128, TT], F32)
            on = sb("on", [128, TT], F32)
            psN = ps("psN3")
            psS = [ps(f"psS{i}") for i in range(5)]
            psQ = [psS[0], psS[1]]
            psOo = [ps("psOb0"), ps("psOb1")]

            w1v = w_in1.rearrange("(k p) n -> p k n", p=128)
            wo1v = w_out1.rearrange("(k p) n -> p k n", p=128)
            for k in range(8):
                S_.dma(lambda e, k=k: e.dma_start(out=wq.t[:, k, :], in_=w1v[:, k, 0:1024]), writes=[wq.b], sem_buf=wq.b, queue="pool")
                S_.dma(lambda e, k=k: e.dma_start(out=wg.t[:, k, :], in_=w1v[:, k, 3072:4096]), writes=[wg.b], sem_buf=wg.b, queue="pool")
                S_.dma(lambda e, k=k: e.dma_start(out=wo1.t[:, k, :], in_=wo1v[:, k, :]), writes=[wo1.b], sem_buf=wo1.b, queue="pool")
            S_.dma(lambda e: e.dma_start(out=identb.t[:], in_=ident_d[:, :]), writes=[identb.b], sem_buf=identb.b, queue="pool")
            S_.dma(lambda e: e.dma_start(out=trib.t[:], in_=tri_d[:, :]), writes=[trib.b], sem_buf=trib.b, queue="pool")
            S_.dma(lambda e: e.dma_start(out=gam_sb.t[:], in_=gam[:, :]), writes=[gam_sb.b], sem_buf=gam_sb.b)
            S_.op("pool", lambda e: e.memset(onesb.t[:], 1.0), writes=[onesb.b])
            S_.op("pool", lambda e: e.memset(qa.t[64:70, :, :], 1.0), writes=[qa.b])
            for i in range(2):
                S_.op("pool", lambda e, i=i: e.memset(ka[i].t[64:70, :], 1.0), writes=[ka[i].b])

            x1ov = x1o.rearrange("(k p) t -> p k t", p=128)
            cpg = cpos.rearrange("h r (t c) -> r h t c", c=512)
            outv = out.rearrange("(k p) t -> p k t", p=128)
            cnt_s = 0
            cnt_p = 0
            xsem = [Buf(f"xsem{i}") for i in range(4)]
            qsem = [Buf(f"qsem{i}") for i in range(4)]
            for G in range(4):
                L = 2048 * (G + 1)
                nkb = 16 * (G + 1)
                S_.dma(lambda e, G=G: e.dma_start(out=X.t[:], in_=x1ov[:, :, G * 512:(G + 1) * 512]), writes=[X.b], sem_buf=X.b)
                S_.op("pool", lambda e: e.tensor_tensor(out=sq.t[:], in0=X.t[:], in1=X.t[:], op=ALU.mult), reads=[X.b], writes=[sq.b])
                for k in range(8):
                    S_.op("act", lambda e, k=k: e.activation(out=xbf.t[:, k, :], in_=X.t[:, k, :], func=AF.Copy,
                                                             scale=gam_sb.t[:, 8 + k:9 + k]),
                          reads=[X.b, gam_sb.b], writes=[xbf.b])

                def mmN(e):
                    for k in range(8):
                        ins = e.matmul(psN.t[:, :], lhsT=onesb.t[:], rhs=sq.t[:, k, :], start=(k == 0), stop=(k == 7))
                    return ins
                S_.op("pe", mmN, reads=[onesb.b, sq.b], writes=[psN.b])
                S_.op("dve", lambda e: e.tensor_scalar(out=rt.t[:], in0=psN.t[:, :], scalar1=1.0 / D, scalar2=EPS,
                                                       op0=ALU.mult, op1=ALU.add), reads=[psN.b], writes=[rt.b])
                S_.op("act", lambda e: e.activation(out=rt.t[:], in_=rt.t[:], func=AF.Sqrt), reads=[rt.b], writes=[rt.b])
                S_.op("dve", lambda e: e.reciprocal(out=rstd.t[:], in_=rt.t[:]), reads=[rt.b], writes=[rstd.b])
                S_.op("dve", lambda e: e.tensor_scalar(out=rq.t[:], in0=rstd.t[:], scalar1=0.125, scalar2=None, op0=ALU.mult),
                      reads=[rstd.b], writes=[rq.b])
                for m in range(8):
                    pq = psQ[m % 2]
                    tqm = tq[m % 2]

                    def mmQ(e, m=m, pq=pq):
                        for k in range(8):
                            ins = e.matmul(pq.t[:, :], lhsT=wq.t[:, k, m * 128:(m + 1) * 128], rhs=xbf.t[:, k, :],
                                           start=(k == 0), stop=(k == 7))
                        return ins
                    S_.op("pe", mmQ, reads=[wq.b, xbf.b], writes=[pq.b])
                    S_.op("dve", lambda e, pq=pq, tqm=tqm: e.tensor_tensor(out=tqm.t[:], in0=pq.t[:, :], in1=rq.t[:], op=ALU.mult),
                          reads=[pq.b, rq.b], writes=[tqm.b])
                    S_.op("act", lambda e, m=m, tqm=tqm: e.activation(out=qa.t[0:64, 2 * m, :], in_=tqm.t[0:64, :], func=AF.Copy),
                          reads=[tqm.b], writes=[qa.b])
                    S_.op("act", lambda e, m=m, tqm=tqm: e.activation(out=qa.t[0:64, 2 * m + 1, :], in_=tqm.t[64:128, :], func=AF.Copy),
                          reads=[tqm.b], writes=[qa.b])
                for a_ in range(4):
                    S_.dma(lambda e, G=G, a_=a_: e.dma_start(out=qa.t[67:70, :, a_ * 128:(a_ + 1) * 128],
                                                             in_=cpg[:, :, 4 * G + a_, 384:512]), writes=[qa.b], sem_buf=qsem[a_])
                for m in range(8):
                    pq = psQ[m % 2]
                    tqm = tq[m % 2]

                    def mmG(e, m=m, pq=pq):
                        for k in range(8):
                            ins = e.matmul(pq.t[:, :], lhsT=wg.t[:, k, m * 128:(m + 1) * 128], rhs=xbf.t[:, k, :],
                                           start=(k == 0), stop=(k == 7))
                        return ins
                    S_.op("pe", mmG, reads=[wg.b, xbf.b], writes=[pq.b])
                    S_.op("dve", lambda e, pq=pq, tqm=tqm: e.tensor_tensor(out=tqm.t[:], in0=pq.t[:, :], in1=rstd.t[:], op=ALU.mult),
                          reads=[pq.b, rstd.b], writes=[tqm.b])
                    S_.op("act", lambda e, m=m, tqm=tqm: e.activation(out=sg.t[:, m, :], in_=tqm.t[:], func=AF.Silu),
                          reads=[tqm.b], writes=[sg.b])
                for h in range(16):
                    kab = ka[h % 2]
                    vab = va[h % 2]
                    S_.dma(lambda e, h=h, kab=kab, L=L: e.dma_start(out=kab.t[0:64, 0:L], in_=ks[h * 64:(h + 1) * 64, 0:L]),
                           writes=[kab.b], sem_buf=kab.b)
                    S_.dma(lambda e, h=h, kab=kab, L=L: e.dma_start(out=kab.t[64:67, 0:L], in_=cneg[h, :, 0:L]),
                           writes=[kab.b], sem_buf=kab.b)
                    S_.dma(lambda e, h=h, vab=vab, nkb=nkb: e.dma_start(out=vab.t[:, 0:nkb, :], in_=vs[h, :, 0:nkb, :]),
                           writes=[vab.b], sem_buf=vab.b)
                    blocks = [(kb, 0, False) for kb in range(16 * G)]
                    for a_ in range(4):
                        for i in range(4):
                            blocks.append((16 * G + 4 * a_ + i, 128 * a_, i == 3))
                    po = psOo[h % 2]
                    nb = len(blocks)
                    LOOK = 2
                    pend = []

                    def emit_pv(item, po=po, vab=vab, nb=nb):
                        (bi, kb, c0, ptt) = item
                        S_.op("pe", lambda e, kb=kb, c0=c0, ptt=ptt, vab=vab, po=po, bi=bi, nb=nb: e.matmul(
                            po.t[:, c0:512], lhsT=vab.t[:, kb, :], rhs=ptt.t[:, c0:512], start=(bi == 0), stop=(bi == nb - 1)),
                            reads=[vab.b, ptt.b], writes=[po.b])

                    for bi, (kb, c0, diag) in enumerate(blocks):
                        pss = psS[cnt_s % len(psS)]
                        cnt_s += 1
                        ptt = pt[cnt_p % len(pt)]
                        cnt_p += 1

                        def mmS(e, kb=kb, c0=c0, diag=diag, pss=pss, kab=kab, h=h):
                            ins = e.matmul(pss.t[:, c0:512], lhsT=kab.t[0:70, kb * 128:(kb + 1) * 128], rhs=qa.t[0:70, h, c0:512],
                                           start=True, stop=not diag)
                            if diag:
                                ins = e.matmul(pss.t[:, c0:c0 + 128], lhsT=identb.t[:], rhs=trib.t[:], start=False, stop=True)
                            return ins
                        S_.op("pe", mmS, reads=[kab.b, qa.b, identb.b, trib.b], writes=[pss.b])
                        S_.op("act", lambda e, c0=c0, pss=pss, ptt=ptt: e.activation(out=ptt.t[:, c0:512], in_=pss.t[:, c0:512], func=AF.Exp),
                              reads=[pss.b], writes=[ptt.b])
                        pend.append((bi, kb, c0, ptt))
                        if len(pend) > LOOK:
                            emit_pv(pend.pop(0))
                    while pend:
                        emit_pv(pend.pop(0))
                    if h % 2 == 0:
                        lo, hi, lo2, hi2 = 0, 64, 64, 128
                    else:
                        lo, hi, lo2, hi2 = 64, 128, 0, 64
                    S_.op("dve", lambda e, po=po, lo2=lo2, hi2=hi2: e.reciprocal(out=rl.t[lo2:hi2, :], in_=po.t[lo2:hi2, :]),
                          reads=[po.b], writes=[rl.b])
                    S_.op("dve", lambda e, po=po, lo=lo, hi=hi, lo2=lo2, hi2=hi2: e.tensor_tensor(
                        out=on.t[lo:hi, :], in0=po.t[lo:hi, :], in1=rl.t[lo2:hi2, :], op=ALU.mult),
                        reads=[po.b, rl.b], writes=[on.b])
                    S_.op("pool", lambda e, h=h, lo=lo, hi=hi: e.tensor_tensor(
                        out=yh.t[lo:hi, h // 2, :], in0=on.t[lo:hi, :], in1=sg.t[lo:hi, h // 2, :], op=ALU.mult),
                        reads=[on.b, sg.b], writes=[yh.b])
                for mo in range(8):
                    pq = psQ[mo % 2]

                    def mmP(e, mo=mo, pq=pq):
                        for c in range(8):
                            ins = e.matmul(pq.t[:, :], lhsT=wo1.t[:, c, mo * 128:(mo + 1) * 128], rhs=yh.t[:, c, :],
                                           start=(c == 0), stop=(c == 7))
                        return ins
                    S_.op("pe", mmP, reads=[wo1.b, yh.b], writes=[pq.b])
                    S_.op("dve", lambda e, mo=mo, pq=pq: e.tensor_tensor(out=X.t[:, mo, :], in0=pq.t[:, :], in1=X.t[:, mo, :], op=ALU.add),
                          reads=[pq.b, X.b], writes=[X.b])
                S_.op("pool", lambda e: e.tensor_tensor(out=sq.t[:], in0=X.t[:], in1=X.t[:], op=ALU.mult), reads=[X.b], writes=[sq.b])
                S_.op("pe", mmN, reads=[onesb.b, sq.b], writes=[psN.b])
                S_.op("dve", lambda e: e.tensor_scalar(out=rt.t[:], in0=psN.t[:, :], scalar1=1.0 / D, scalar2=EPS,
                                                       op0=ALU.mult, op1=ALU.add), reads=[psN.b], writes=[rt.b])
                S_.op("act", lambda e: e.activation(out=rt.t[:], in_=rt.t[:], func=AF.Sqrt), reads=[rt.b], writes=[rt.b])
                S_.op("dve", lambda e: e.reciprocal(out=rstd.t[:], in_=rt.t[:]), reads=[rt.b], writes=[rstd.b])
                for mo in range(8):
                    S_.op("dve", lambda e, mo=mo: e.scalar_tensor_tensor(out=X.t[:, mo, :], in0=X.t[:, mo, :],
                                                                        scalar=gam_sb.t[:, 16 + mo:17 + mo], in1=rstd.t[:],
                                                                        op0=ALU.mult, op1=ALU.mult),
                          reads=[X.b, gam_sb.b, rstd.b], writes=[X.b])
                S_.dma(lambda e, G=G: e.dma_start(out=outv[:, :, G * 512:(G + 1) * 512], in_=X.t[:]), reads=[X.b], sem_buf=X.b)
            S_.flush()
    return nc


def make_in_maps(x, norm_g, final_g, lru_w_in, lru_conv_w, lru_conv_b, lru_wa, lru_ba, lru_wx, lru_bx,
                 lru_a_param, lru_w_out, fox_w_in, fox_b_f, fox_w_out):
    f = np.float32
    x = np.asarray(x, f)

    def col(v):
        return np.ascontiguousarray(np.asarray(v, f).reshape(-1, 128).T)

    gam = np.concatenate([col(norm_g[0]), col(norm_g[1]), col(final_g)], axis=1)
    cwv = np.asarray(lru_conv_w[0], f)
    cw = np.concatenate([col(cwv[k]) for k in range(4)], axis=1)
    vec = np.concatenate([col(lru_conv_b[0]), col(lru_ba[0]), col(lru_bx[0]), col(lru_a_param[0])], axis=1)
    ident = np.eye(128, dtype=f)
    kk = np.arange(128)[:, None]
    cc = np.arange(128)[None, :]
    tri = np.where(kk <= cc, 0.0, NEG).astype(f)
    common = {
        "gam": np.ascontiguousarray(gam), "w_in0": np.ascontiguousarray(lru_w_in[0], f), "cw": np.ascontiguousarray(cw),
        "vec": np.ascontiguousarray(vec), "wa": np.ascontiguousarray(lru_wa[0], f), "wx": np.ascontiguousarray(lru_wx[0], f),
        "w_out0": np.ascontiguousarray(lru_w_out[0], f), "w_in1": np.ascontiguousarray(fox_w_in[0], f),
        "bfv": np.ascontiguousarray(np.asarray(fox_b_f[0], f).reshape(16, 1)),
        "w_out1": np.ascontiguousarray(fox_w_out[0], f), "ident": ident, "tri": tri,
    }
    maps = []
    for b in range(2):
        xbT = np.ascontiguousarray(x[b].T)
        for j in range(4):
            P = 128 * (3 - j)
            xT = np.zeros((D, S), f)
            xT[:, P:] = xbT[:, :S - P]
            tokmask = np.ones((128, 512), f)
            tokmask[:, :P] = 0.0
            lsub = np.zeros((16, 512), f)
            if P > 0:
                lsub[:, 0] = NEG
                lsub[:, P] = -NEG
            m = dict(common)
            m.update({"xT": xT, "tokmask": tokmask, "lsub": lsub})
            maps.append(m)
    return maps


_NC_CACHE = {}


def kernel(**inputs):
    maps = make_in_maps(**inputs)
    if "nc" not in _NC_CACHE:
        _NC_CACHE["nc"] = build()
    nc = _NC_CACHE["nc"]
    res = run_bass_kernel_spmd(nc, maps, core_ids=list(range(8)))
    outp = np.zeros((2, S, D), np.float32)
    for b in range(2):
        for j in range(4):
            o = np.asarray(res.results[b * 4 + j]["out"])
            for s in range(16):
                t = 128 * (4 * s + j)
                outp[b, t:t + 128, :] = o[:, s * 128:(s + 1) * 128].T
    return outp
```
